# Optimizing a Trainium2 kernel written in Bass

```python
import jax, jax.numpy as jnp
from jax import lax
import numpy as np

D_MODEL = 1024
BATCH = 8
SEQ = 4096
DEPTH = 2

HEAD_DIM = 64
N_MIXERS = 4
GROUP_WIDTH = D_MODEL // N_MIXERS
N_GROUP_HEADS = GROUP_WIDTH // HEAD_DIM
MIX_WIDTH = N_MIXERS * GROUP_WIDTH
N_IN_SLICES = 13
IN_WIDTH = N_IN_SLICES * GROUP_WIDTH
DILATED_PAIRS = ((128, 1), (512, 4), (2048, 16))
ATT_BLOCK = 128
SGU_CHUNK = 128
POOL_SIZES = (2, 4, 8, 16)
POOL_CH = GROUP_WIDTH // len(POOL_SIZES)
RET_CHUNK = 128
NORM_EPS = 1e-6

kernel_name = "hymba_hybrid_dilated_sgu_pool_retention"


def _rms_norm(x, g):
    xf = x.astype(jnp.float32)
    y = xf * lax.rsqrt(jnp.mean(xf * xf, axis=-1, keepdims=True) + NORM_EPS)
    return (y * g.astype(jnp.float32)).astype(x.dtype)


def _layer_norm(x, g):
    xf = x.astype(jnp.float32)
    mu = jnp.mean(xf, axis=-1, keepdims=True)
    var = jnp.mean(jnp.square(xf - mu), axis=-1, keepdims=True)
    return ((xf - mu) * lax.rsqrt(var + NORM_EPS) * g.astype(jnp.float32)).astype(x.dtype)


def _alibi_slopes(n):
    return jnp.exp2(-8.0 * (jnp.arange(n, dtype=jnp.float32) + 1.0) / n)


def _dilated_branch(q, k, v, window, dil, slopes):
    b, h, s, hd = q.shape
    span = dil * ATT_BLOCK
    s_pad = -(-s // span) * span
    L = s_pad // dil
    nb = L // ATT_BLOCK

    def stride_blocks(t):
        t = jnp.pad(t, ((0, 0), (0, 0), (0, s_pad - s), (0, 0))).reshape(b, h, L, dil, hd)
        return jnp.swapaxes(t, 2, 3).reshape(b, h, dil, nb, ATT_BLOCK, hd)

    def with_prev(t):
        prev = jnp.pad(t[:, :, :, :-1], ((0, 0), (0, 0), (0, 0), (1, 0), (0, 0), (0, 0)))
        return jnp.concatenate([prev, t], axis=4)

    qb = stride_blocks(q)
    kc = with_prev(stride_blocks(k))
    vc = with_prev(stride_blocks(v))
    scores = jnp.einsum('bhrnqd,bhrnkd->bhrnqk', qb, kc).astype(jnp.float32) * (hd ** -0.5)
    qi = jnp.arange(ATT_BLOCK)[:, None]
    ki = jnp.arange(2 * ATT_BLOCK)[None, :]
    dist = qi + ATT_BLOCK - ki
    band = (dist >= 0) & (dist <= window // dil)
    valid = (jnp.arange(nb)[:, None] * ATT_BLOCK + ki - ATT_BLOCK) >= 0
    mask = band[None, :, :] & valid[:, None, :]
    bias = -slopes[:, None, None, None, None] * (dist * dil).astype(jnp.float32)
    scores = jnp.where(mask, scores + bias, -jnp.inf)
    m = jnp.max(scores, axis=-1, keepdims=True)
    p = jnp.exp(scores - m)
    den = jnp.sum(p, axis=-1, keepdims=True)
    o = jnp.einsum('bhrnqk,bhrnkd->bhrnqd', (p / den).astype(v.dtype), vc)
    lse = (m + jnp.log(den))[..., 0]

    def unstride(t):
        tail = t.shape[5:]
        t = t.reshape(b, h, dil, L, *tail)
        return jnp.swapaxes(t, 2, 3).reshape(b, h, s_pad, *tail)[:, :, :s]

    return unstride(o), unstride(lse)


def _dilated_mixture(q, k, v):
    slopes = _alibi_slopes(q.shape[1])
    outs, lses = [], []
    for window, dil in DILATED_PAIRS:
        o, lse = _dilated_branch(q, k, v, window, dil, slopes)
        outs.append(o)
        lses.append(lse)
    w = jax.nn.softmax(jnp.stack(lses), axis=0)
    o = jnp.einsum('gbhs,gbhsd->bhsd', w.astype(v.dtype), jnp.stack(outs))
    b, h, s, hd = o.shape
    return o.transpose(0, 2, 1, 3).reshape(b, s, h * hd)


def _spatial_gating(u, v, norm_g, w_s, b_s):
    b, s, _ = u.shape
    v = _layer_norm(v, norm_g)
    nc = s // SGU_CHUNK
    v = v.reshape(b, nc, SGU_CHUNK, N_GROUP_HEADS, HEAD_DIM)
    causal = jnp.tril(jnp.ones((SGU_CHUNK, SGU_CHUNK), dtype=bool))
    w = jnp.where(causal, w_s, 0.0)
    mixed = jnp.einsum('gts,bcsgd->bctgd', w, v) + b_s.T[:, :, None]
    return u * mixed.reshape(b, s, GROUP_WIDTH)


def _multiscale_pool(xc, pool_w, pool_scale):
    b, s, _ = xc.shape
    xg = xc.reshape(b, s, len(POOL_SIZES), POOL_CH)
    csum = jnp.pad(lax.cumsum(xg.astype(jnp.float32), axis=1), ((0, 0), (1, 0), (0, 0), (0, 0)))
    t = jnp.arange(s)
    pooled = []
    for g, p in enumerate(POOL_SIZES):
        lo = jnp.maximum(t + 1 - p, 0)
        win_sum = csum[:, 1:, g] - csum[:, lo, g]
        cnt = jnp.minimum(t + 1, p).astype(jnp.float32)
        pooled.append(win_sum / cnt[None, :, None])
    pooled = jnp.stack(pooled, axis=2).astype(xc.dtype) - xg
    y = jnp.einsum('bsgc,gcd->bsgd', pooled, pool_w)
    return y.reshape(b, s, GROUP_WIDTH) * pool_scale


def _retention(q, k, v, norm_g):
    b, s, _ = q.shape
    H, C = N_GROUP_HEADS, RET_CHUNK
    n = s // C

    def chunks(t):
        return t.reshape(b, n, C, H, HEAD_DIM).transpose(0, 3, 1, 2, 4)

    q, k, v = chunks(q), chunks(k) * (HEAD_DIM ** -0.5), chunks(v)
    log_g = jnp.log(1.0 - jnp.exp2(-5.0 - jnp.arange(H, dtype=jnp.float32)))
    i = jnp.arange(C, dtype=jnp.float32)
    diff = i[:, None] - i[None, :]
    decay = jnp.where(diff >= 0, jnp.exp(log_g[:, None, None] * jnp.maximum(diff, 0.0)), 0.0)
    zeta = jnp.exp(log_g[:, None] * (C - 1 - i))
    xi = jnp.exp(log_g[:, None] * (i + 1))
    chunk_decay = jnp.exp(log_g * C)
    inner = jnp.einsum('bhnid,bhnjd->bhnij', q, k) * decay[:, None].astype(q.dtype)
    inner = jnp.einsum('bhnij,bhnje->bhnie', inner, v)
    kv = jnp.einsum('bhnjd,bhnje->nbhde', k * zeta[:, None, :, None].astype(k.dtype), v).astype(jnp.float32)

    def step(state, kv_n):
        return state * chunk_decay[None, :, None, None] + kv_n, state

    _, prev = lax.scan(step, jnp.zeros_like(kv[0]), kv)
    cross = jnp.einsum('bhnid,nbhde->bhnie', q * xi[:, None, :, None].astype(q.dtype), prev.astype(q.dtype))
    of = (inner + cross).astype(jnp.float32)
    mu = jnp.mean(of, axis=-1, keepdims=True)
    var = jnp.mean(jnp.square(of - mu), axis=-1, keepdims=True)
    on = (of - mu) * lax.rsqrt(var + NORM_EPS)
    on = on.transpose(0, 2, 3, 1, 4).reshape(b, s, GROUP_WIDTH)
    return (on * norm_g.astype(jnp.float32)).astype(v.dtype)


def _layer(x, pre_g, w_in, sgu_g, sgu_w, sgu_b, pool_w, pool_scale, ret_g, w_out, post_g):
    b, s, _ = x.shape
    h = _rms_norm(x, pre_g)
    proj = jnp.einsum('bsd,de->bse', h, w_in)
    aq, ak, av, ag, bu, bv, bg, cx, cg, dq, dk, dv, dg = jnp.split(proj, N_IN_SLICES, axis=-1)

    def heads(t):
        return t.reshape(b, s, N_GROUP_HEADS, HEAD_DIM).transpose(0, 2, 1, 3)

    ya = _dilated_mixture(heads(aq), heads(ak), heads(av))
    yb = _spatial_gating(bu, bv, sgu_g, sgu_w, sgu_b)
    yc = _multiscale_pool(cx, pool_w, pool_scale)
    yd = _retention(dq, dk, dv, ret_g)
    y = jnp.concatenate([ya * jax.nn.silu(ag), yb * jax.nn.silu(bg),
                         yc * jax.nn.silu(cg), yd * jax.nn.silu(dg)], axis=-1)
    y = jnp.einsum('bse,ed->bsd', y, w_out)
    return x + _rms_norm(y, post_g).astype(x.dtype)


def setup_inputs(seed: int = 0) -> dict:
    key = jax.random.key(seed)
    ks = jax.random.split(key, 12)
    f32 = jnp.float32
    nrm = lambda k, shape: jax.random.normal(k, shape, f32)
    return {
        "x": nrm(ks[0], (BATCH, SEQ, D_MODEL)),
        "pre_g": 1.0 + 0.05 * nrm(ks[1], (DEPTH, D_MODEL)),
        "w_in": nrm(ks[2], (DEPTH, D_MODEL, IN_WIDTH)) * D_MODEL ** -0.5,
        "sgu_g": 1.0 + 0.05 * nrm(ks[3], (DEPTH, GROUP_WIDTH)),
        "sgu_w": nrm(ks[4], (DEPTH, N_GROUP_HEADS, SGU_CHUNK, SGU_CHUNK)) * SGU_CHUNK ** -0.5,
        "sgu_b": 1.0 + 0.05 * nrm(ks[5], (DEPTH, N_GROUP_HEADS, SGU_CHUNK)),
        "pool_w": nrm(ks[6], (DEPTH, len(POOL_SIZES), POOL_CH, POOL_CH)) * POOL_CH ** -0.5,
        "pool_scale": 1.0 + 0.1 * nrm(ks[7], (DEPTH, GROUP_WIDTH)),
        "ret_g": 1.0 + 0.05 * nrm(ks[8], (DEPTH, GROUP_WIDTH)),
        "w_out": nrm(ks[9], (DEPTH, MIX_WIDTH, D_MODEL)) * MIX_WIDTH ** -0.5,
        "post_g": 1.0 + 0.05 * nrm(ks[10], (DEPTH, D_MODEL)),
    }


def reference(x, pre_g, w_in, sgu_g, sgu_w, sgu_b, pool_w, pool_scale, ret_g, w_out, post_g):
    for l in range(DEPTH):
        x = _layer(x, pre_g[l], w_in[l], sgu_g[l], sgu_w[l], sgu_b[l], pool_w[l],
                   pool_scale[l], ret_g[l], w_out[l], post_g[l])
    return x
```

```python
import math
from contextlib import ExitStack

import numpy as np
import concourse.bass as bass
import concourse.mybir as mybir
from concourse.bass_utils import run_bass_kernel_spmd

F32 = mybir.dt.float32
BF16 = mybir.dt.bfloat16
U8 = mybir.dt.uint8
AF = mybir.ActivationFunctionType
ALU = mybir.AluOpType
AX = mybir.AxisListType

D = 1024
S = 4096
NT = S // 128
NST = S // 512
DEPTH = 2
EPS = 1e-6
DILS = (1, 4, 16)
POOLS = (2, 4, 8, 16)
COLMAP = [0, 1, 2, 3, 4, 6, 8, 9, 10, 12, 5, 7, 10, 11]
NCOL = len(COLMAP) * 256
NV = 37
SEM_LIMIT = 30000
NORM_AHEAD = True


class Op:
    __slots__ = ("eng", "fn", "reads", "writes", "dma", "deps", "idx", "ev", "ndma", "needs_inc")


class Prog:
    ENGS = ("pe", "act", "dve", "pool", "sp")

    def __init__(self):
        self.ops = []
        self.last_w = {}
        self.readers = {}
        self.last_on_eng = {}
        self.open_dmas = []
        self.barrier_idx = None
        self.rec = None

    def replay(self, items):
        assert self.rec is None
        for a in items:
            self.op(*a)

    def op(self, eng, fn, reads=(), writes=(), dma=None, ndma=1):
        if self.rec is not None:
            self.rec.append((eng, fn, tuple(reads), tuple(writes), dma, ndma))
            return None
        o = Op()
        writes = tuple(writes) + tuple(r for r in reads if r.startswith("ps"))
        reads = tuple(r for r in reads if not r.startswith("ps"))
        o.eng, o.fn, o.reads, o.writes, o.dma, o.ndma = eng, fn, tuple(reads), tuple(writes), dma, ndma
        o.ev = None
        o.needs_inc = False
        deps = set()
        for r in o.reads:
            w = self.last_w.get(r)
            if w is not None:
                deps.add(w)
        for w_ in o.writes:
            w = self.last_w.get(w_)
            if w is not None:
                deps.add(w)
            deps.update(self.readers.get(w_, ()))
        if self.barrier_idx is not None:
            deps.add(self.barrier_idx)
        o.idx = len(self.ops)
        o.deps = deps
        for r in o.reads:
            self.readers.setdefault(r, []).append(o.idx)
        for w_ in o.writes:
            self.last_w[w_] = o.idx
            self.readers[w_] = []
        self.ops.append(o)
        if dma is None:
            self.last_on_eng[eng] = o.idx
        else:
            self.open_dmas.append(o.idx)
        return o

    def barrier(self, fn):
        o = self.op("dve", fn)
        o.deps = set(self.last_on_eng.values()) | set(self.open_dmas)
        o.deps.discard(o.idx)
        self.open_dmas = []
        self.barrier_idx = o.idx
        self.last_w = {}
        self.readers = {}
        return o

    def finalize(self, final_wait_keys=()):
        ops = self.ops
        for o in ops:
            for d in o.deps:
                src = ops[d]
                if src.dma is None and src.eng == "pe" and o.eng == "pe" and o.dma is None:
                    continue
                src.needs_inc = True
        finals = [o for o in ops if o.dma is not None and any(w in final_wait_keys for w in o.writes)]
        eng_cnt = {e: 0 for e in self.ENGS}
        eng_epoch = {e: 0 for e in self.ENGS}
        dma_cnt = {}
        sem_names = set()
        for o in ops:
            if o.dma is not None:
                sname = "d_%s" % (o.dma,)
                dma_cnt[sname] = dma_cnt.get(sname, 0) + 16 * o.ndma
                o.ev = (sname, dma_cnt[sname])
                sem_names.add(sname)
            elif o.needs_inc:
                if eng_cnt[o.eng] >= SEM_LIMIT:
                    eng_epoch[o.eng] += 1
                    eng_cnt[o.eng] = 0
                eng_cnt[o.eng] += 1
                sname = "e_%s_%d" % (o.eng, eng_epoch[o.eng])
                o.ev = (sname, eng_cnt[o.eng])
                sem_names.add(sname)
        self.finals = finals
        return sorted(sem_names)

    def emit(self, nc, sems):
        ops = self.ops
        per_eng = {e: [] for e in self.ENGS}
        for o in ops:
            per_eng[o.eng].append(o)
        finals = self.finals

        def run_stream(engname, eng):
            waited = {}
            for o in per_eng[engname]:
                need = {}
                for d in o.deps:
                    src = ops[d]
                    if src.ev is None:
                        continue
                    if src.dma is None and src.eng == "pe" and engname == "pe" and o.dma is None:
                        continue
                    s, v = src.ev
                    if need.get(s, 0) < v:
                        need[s] = v
                for s, v in need.items():
                    if waited.get(s, 0) >= v:
                        continue
                    eng.wait_ge(sems[s], v)
                    waited[s] = v
                if o.dma is not None:
                    insts = o.fn(eng)
                    if not isinstance(insts, (list, tuple)):
                        insts = [insts]
                    assert len(insts) == o.ndma, (len(insts), o.ndma)
                    for ins in insts:
                        ins.then_inc(sems[o.ev[0]], 16)
                else:
                    ins = o.fn(eng)
                    if o.needs_inc:
                        ins.then_inc(sems[o.ev[0]], 1)
            if engname == "sp":
                need = {}
                for o in finals:
                    s, v = o.ev
                    if need.get(s, 0) < v:
                        need[s] = v
                for s, v in need.items():
                    eng.wait_ge(sems[s], v)

        with nc.Block() as block:
            @block.sync
            def _(e):
                run_stream("sp", e)

            @block.tensor
            def _(e):
                run_stream("pe", e)

            @block.scalar
            def _(e):
                run_stream("act", e)

            @block.vector
            def _(e):
                run_stream("dve", e)

            @block.gpsimd
            def _(e):
                run_stream("pool", e)


def interleave(lists):
    items = []
    for li, L in enumerate(lists):
        n = len(L)
        for m, a in enumerate(L):
            items.append(((m + 0.5) / n, li, m, a))
    items.sort(key=lambda z: (z[0], z[1], z[2]))
    return [z[3] for z in items]


class Arena:
    def __init__(self, ap_u8, nbytes):
        self.ap = ap_u8
        self.n = nbytes
        self.off = 0

    def reset(self):
        self.off = 0

    def take(self, shape, dt):
        esz = 4 if dt == F32 else 2
        n = 1
        for s_ in shape[1:]:
            n *= s_
        nb = (n * esz + 31) // 32 * 32
        assert self.off + nb <= self.n, ("arena overflow", self.off, nb, self.n)
        v = self.ap[:, self.off:self.off + n * esz].bitcast(dt)
        self.off += nb
        if len(shape) == 3:
            v = v.rearrange("p (a b) -> p a b", b=shape[2])
        elif len(shape) == 4:
            v = v.rearrange("p (a b c) -> p a b c", b=shape[2], c=shape[3])
        return v


def build_program(depth=DEPTH, stop=None, p3level=4):
    nc = bass.Bass("TRN2", target_bir_lowering=False)

    def din(name, shape):
        return nc.dram_tensor(name, list(shape), F32, kind="ExternalInput").ap()

    x_d = din("x", [S, D])
    win_d = din("w_in", [DEPTH, D, 13 * 256])
    wout_d = din("w_out", [DEPTH, D, D])
    sguwT_d = din("sgu_wT", [DEPTH, 4, 128, 128])
    sgub_d = din("sgu_b", [DEPTH, 4, 128])
    poolw_d = din("pool_w", [DEPTH, 4, 64, 64])
    postg_d = din("post_g", [DEPTH, D])
    vecs_d = din("vecs", [128, NV])
    ident_d = din("ident", [128, 128])
    tri_d = din("triT", [128, 128])
    poolM_d = din("poolM", [12, 128, 128])
    etab_d = din("etab", [12, 128, 256])
    rm_d = din("rm", [128, 128])
    g128_d = din("g128", [128, 256])
    out_d = nc.dram_tensor("out", [S, D], F32, kind="ExternalOutput").ap()
    if stop is not None:
        dbg_d = nc.dram_tensor("dbg", [4, 128, 2 * S], BF16, kind="ExternalOutput").ap()

    x_t = x_d.rearrange("(t p) d -> t p d", p=128)
    out_t = out_d.rearrange("(t p) d -> t p d", p=128)

    es = ExitStack()
    with es:
        def sb(name, shape, dt):
            return es.enter_context(nc.sbuf_tensor(name, shape, dt))

        win_bf = sb("win_bf", [128, 8, NCOL], BF16)
        wout_bf = sb("wout_bf", [128, 8, D], BF16)
        yaT = sb("yaT", [128, 2, S], BF16)
        identb = sb("identb", [128, 128], BF16)
        identf = sb("identf", [128, 128], F32)
        vecs = sb("vecs_sb", [128, NV], F32)
        postg = sb("postg", [128, D], F32)
        scr = sb("scr", [128, 8], F32)
        A_BYTES = 69632
        B_BYTES = 45056
        arA_t = sb("arenaA", [128, A_BYTES], U8)
        arB_t = sb("arenaB", [128, B_BYTES], U8)
        arA = Arena(arA_t, A_BYTES)
        arB = Arena(arB_t, B_BYTES)
        banks = [es.enter_context(nc.psum_tensor("bank%d" % i, [128, 512], F32)) for i in range(8)]

        def bk(i):
            return banks[i][:, :]

        def bkbf(i):
            return banks[i][:, :].bitcast(BF16)

        P = Prog()

        def mm(out, lhsT, rhs, start=True, stop=True, r=(), w=()):
            P.op("pe", lambda e: e.matmul(out, lhsT=lhsT, rhs=rhs, start=start, stop=stop), r, w)

        def tr(out, in_, ident, r=(), w=()):
            P.op("pe", lambda e: e.transpose(out=out, in_=in_, identity=ident), r, w)

        def act(out, in_, func, r=(), w=(), scale=1.0, accum=None):
            if accum is None:
                P.op("act", lambda e: e.activation(out=out, in_=in_, func=func, scale=scale), r, w)
            else:
                P.op("act", lambda e: e.activation(out=out, in_=in_, func=func, scale=scale, accum_out=accum), r, w)

        def tt(eng, out, in0, in1, op, r=(), w=()):
            P.op(eng, lambda e: e.tensor_tensor(out=out, in0=in0, in1=in1, op=op), r, w)

        def ts(out, in0, s1, s2, op0, op1=None, r=(), w=()):
            if op1 is None:
                P.op("dve", lambda e: e.tensor_scalar(out=out, in0=in0, scalar1=s1, scalar2=None, op0=op0), r, w)
            else:
                P.op("dve", lambda e: e.tensor_scalar(out=out, in0=in0, scalar1=s1, scalar2=s2, op0=op0, op1=op1), r, w)

        def stt(out, in0, scalar, in1, op0, op1, r=(), w=()):
            P.op("dve", lambda e: e.scalar_tensor_tensor(out=out, in0=in0, scalar=scalar, in1=in1, op0=op0, op1=op1), r, w)

        def cp(eng, out, in_, r=(), w=()):
            if eng == "act":
                P.op("act", lambda e: e.activation(out=out, in_=in_, func=AF.Copy), r, w)
            else:
                P.op(eng, lambda e: e.tensor_copy(out=out, in_=in_), r, w)

        def red(out, in_, r=(), w=()):
            P.op("dve", lambda e: e.tensor_reduce(out=out, in_=in_, axis=AX.X, op=ALU.add), r, w)

        def dma(eng, out, in_, r=(), w=(), key=None):
            P.op(eng, lambda e: e.dma_start(out=out, in_=in_), r, w, dma=key)

        def memset(eng, ap, val, w=()):
            P.op(eng, lambda e: e.memset(ap, val), (), w)

        def barrier():
            P.barrier(lambda e: e.memset(scr[:, 0:1], 0.0))

        m05 = vecs[:, 28:29]

        dma("sp", identf[:], ident_d, w=["identf"], key="identf")
        dma("sp", vecs[:], vecs_d, w=["vecs"], key="vecs")
        cp("dve", identb[:], identf[:], r=["identf"], w=["identb"])

        rr = {"i": 0}

        def alt(engs):
            rr["i"] += 1
            return engs[rr["i"] % len(engs)]

        def norm_a(xs, xkey, h_ap, hkey, junk, small, col):
            ss = small[:, col:col + 1]
            tmp = small[:, 8 + col:9 + col]
            rstd = small[:, 16 + col:17 + col]
            skey = "small%d" % col
            act(junk, xs, AF.Square, r=[xkey, "smallz"], w=["junk", skey], accum=ss)
            ts(tmp, ss, 1.0 / D, EPS, ALU.mult, ALU.add, r=[skey], w=[skey + "t"])
            tt("pool", rstd, tmp, m05, ALU.pow, r=[skey + "t", "vecs"], w=[skey + "r"])
            ts(h_ap, xs, rstd, None, ALU.mult, r=[xkey, skey + "r"], w=[hkey])

        def norm_b(h_ap, hkey, hT, hTkey, j, tbank):
            pT = bkbf(tbank).rearrange("p (k t) -> p k t", t=128)
            for k in range(8):
                tr(pT[:, k, :], h_ap[:, k * 128:(k + 1) * 128], identb[:], r=[hkey, "identb"], w=["ps%d" % tbank])
            cp("act", hT[:, :, j * 128:(j + 1) * 128], pT, r=["ps%d" % tbank], w=[hTkey])

        def norm_transpose(xs, xkey, h_ap, hkey, junk, small, col, hT, hTkey, j, tbank):
            norm_a(xs, xkey, h_ap, hkey, junk, small, col)
            norm_b(h_ap, hkey, hT, hTkey, j, tbank)

        def gate_evac(ps_ap, pskey, ttmp, out_ap, outkey):
            act(ttmp, ps_ap, AF.Tanh, r=[pskey], w=["ttmp"], scale=0.5)
            stt(out_ap, ttmp, 1.0, ps_ap, ALU.add, ALU.mult, r=["ttmp", pskey], w=[outkey])

        for l in range(depth):
            def xin_tile(t):
                return x_t[t] if l == 0 else out_t[t]

            def xin_keys(t):
                return [] if l == 0 else ["od%d" % t]

            barrier()
            arA.reset()
            arB.reset()
            stages = [arA.take([128, 8, 256], F32) for _ in range(4)]
            pregb = vecs[:, l * 8:(l + 1) * 8].unsqueeze(2).to_broadcast([128, 8, 256])
            nload = 0
            for i, src in enumerate(COLMAP):
                st = stages[nload % 4]
                sk = "stage%d" % (nload % 4)
                nload += 1
                dma("sp", st, win_d[l, :, src * 256:(src + 1) * 256].rearrange("(k p) n -> p k n", p=128),
                    w=[sk], key=sk)
                tt(alt(["dve", "pool"]), win_bf[:, :, i * 256:(i + 1) * 256], st, pregb, ALU.mult,
                   r=[sk, "vecs"], w=["win_bf"])
            for i in range(4):
                st = stages[nload % 4]
                sk = "stage%d" % (nload % 4)
                nload += 1
                dma("sp", st, wout_d[l, :, i * 256:(i + 1) * 256].rearrange("(k p) n -> p k n", p=128),
                    w=[sk], key=sk)
                act(wout_bf[:, :, i * 256:(i + 1) * 256], st, AF.Copy, r=[sk], w=["wout_bf"], scale=0.5)
            dma("sp", postg[:], postg_d[l].partition_broadcast(128), w=["postg"], key="postg")

            barrier()
            arA.reset()
            arB.reset()
            qT = arA.take([128, 2, S], BF16)
            kT = arA.take([128, 2, S], BF16)
            vT = arA.take([128, 2, S], BF16)
            xts = [arB.take([128, D], F32) for _ in range(4)]
            hs = [arB.take([128, D], BF16) for _ in range(3)]
            hTs = [arB.take([128, 8, 512], BF16) for _ in range(2)]
            junk = arB.take([128, D], BF16)
            ttmp = arB.take([128, 512], F32)
            small = arB.take([128, 24], F32)
            memset("pool", small, 0.0, w=["smallz"])
            nchunk = 0

            p1 = {"a": 0}

            def p1_ensure_a(upto):
                while p1["a"] <= upto and p1["a"] < NT:
                    t_ = p1["a"]
                    xkey_ = "xt%d" % (t_ % 4)
                    dma("sp", xts[t_ % 4], xin_tile(t_), r=xin_keys(t_), w=[xkey_], key=xkey_)
                    norm_a(xts[t_ % 4], xkey_, hs[t_ % 3], "h%d" % (t_ % 3), junk, small, t_ % 4)
                    p1["a"] += 1

            def p1_b(t_):
                p1_ensure_a(t_ + 2)
                norm_b(hs[t_ % 3], "h%d" % (t_ % 3), hTs[(t_ // 4) % 2], "hT%d" % ((t_ // 4) % 2), t_ % 4, t_ % 2)

            p1_ensure_a(1)
            for j in range(4):
                p1_b(j)
            for s in range(NST):
                hT = hTs[s % 2]
                hTkey = "hT%d" % (s % 2)
                tok = slice(s * 512, (s + 1) * 512)
                nloc = 0
                for sl in range(4):
                    for c in range(2):
                        bi = 2 + nchunk % 3
                        nchunk += 1
                        pk = "ps%d" % bi
                        col0 = sl * 256 + c * 128
                        for k in range(8):
                            mm(bk(bi), win_bf[:, k, col0:col0 + 128], hT[:, k, :], start=(k == 0), stop=(k == 7),
                               r=["win_bf", hTkey], w=[pk])
                        if sl < 3:
                            dst = (qT, kT, vT)[sl]
                            cp(alt(["act", "dve"]), dst[:, c, tok], bk(bi), r=[pk], w=["qkv"])
                        else:
                            gate_evac(bk(bi), pk, ttmp, yaT[:, c, tok], "yaT")
                        nloc += 1
                        if s + 1 < NST and nloc % 2 == 0:
                            p1_b(4 * (s + 1) + nloc // 2 - 1)

            if stop == (l, 1):
                barrier()
                for i_, t_ in enumerate((qT, kT, vT, yaT)):
                    dma("sp", dbg_d[i_].rearrange("p (c s) -> p c s", c=2), t_ if i_ < 3 else t_[:],
                        w=["dbg"], key="dbg")
                break
            barrier()
            arB.reset()
            etab = arB.take([128, 12, 256], F32)
            acc = arB.take([128, S], F32)
            NSB = 4
            NVB = 6
            SBANKS = (1, 2, 5)
            OBANKS = (3, 4, 6, 7)
            LA = 2
            AD = 2
            ess = [arB.take([128, 2, 128], F32) for _ in range(NSB)]
            pTs = [arB.take([128, 2, 128], BF16) for _ in range(NSB)]
            vbl = [arB.take([128, 80], BF16) for _ in range(NVB)]
            onesf = arB.take([128, 64], F32)
            obf = arB.take([128, 512], BF16)
            dma("sp", etab, etab_d.rearrange("e p n -> p e n"), w=["etab"], key="etab")
            memset("pool", onesf, 1.0, w=["onesf"])
            for i in range(NVB):
                memset("pool", vbl[i], 1.0, w=["vbl%d" % i])
            it = 0
            for h in range(4):
                c = h // 2
                b0 = 64 * (h % 2)
                qh = qT[b0:b0 + 64, c, :]
                kh = kT[b0:b0 + 64, c, :]
                vh = vT[b0:b0 + 64, c, :]
                idh = identb[b0:b0 + 64, b0:b0 + 64]
                its = []
                for g, dil in enumerate(DILS):
                    nb = NT // dil
                    for r_ in range(dil):
                        for n in range(nb):
                            its.append((g, dil, r_, n))
                views = {}
                for g, dil in enumerate(DILS):
                    views[g] = (
                        qh.rearrange("p (n i r) -> p r n i", r=dil, i=128),
                        kh.rearrange("p (n i r) -> p r n i", r=dil, i=128),
                        vh.rearrange("p (n i r) -> p r n i", r=dil, i=128),
                        acc[0:65, :].rearrange("p (n i r) -> p r n i", r=dil, i=128),
                        etab[:, h * 3 + g, :].rearrange("p (a q) -> p a q", q=128),
                    )

                def emit_ts(k, itk):
                    g, dil, r_, n = its[k]
                    qv, kv_, vv, av, et = views[g]
                    vs = itk % NVB
                    psV = bkbf(0)[:, 0:64]
                    tr(psV, vv[:, r_, n, :], idh, r=["qkv", "identb"], w=["ps0"])
                    cp("act", vbl[vs][:, 0:64], psV, r=["ps0"], w=["vbl%d" % vs])
                    sbi = SBANKS[itk % 3]
                    skey = "ps%d" % sbi
                    psS = bk(sbi)[:, 0:256].rearrange("p (a q) -> p a q", q=128)
                    if n > 0:
                        mm(psS[:, 0, :], kv_[:, r_, n - 1, :], qv[:, r_, n, :], r=["qkv"], w=[skey])
                    mm(psS[:, 1, :], kv_[:, r_, n, :], qv[:, r_, n, :], r=["qkv"], w=[skey])

                def emit_soft(k, itk):
                    g, dil, r_, n = its[k]
                    qv, kv_, vv, av, et = views[g]
                    sbi = SBANKS[itk % 3]
                    skey = "ps%d" % sbi
                    psS = bk(sbi)[:, 0:256].rearrange("p (a q) -> p a q", q=128)
                    lo = 0 if n > 0 else 1
                    e_ = ess[itk % NSB]
                    p_ = pTs[itk % NSB]
                    act(e_[:, lo:2, :], psS[:, lo:2, :], AF.Exp, r=[skey], w=["es%d" % (itk % NSB)], scale=0.125)
                    tt("dve", p_[:, lo:2, :], e_[:, lo:2, :], et[:, lo:2, :], ALU.mult,
                       r=["es%d" % (itk % NSB), "etab"], w=["pT%d" % (itk % NSB)])

                def emit_pv(k, itk):
                    g, dil, r_, n = its[k]
                    p_ = pTs[itk % NSB]
                    pkey = "pT%d" % (itk % NSB)
                    vs = itk % NVB
                    obi = OBANKS[itk % 4]
                    okey = "ps%d" % obi
                    psO = bk(obi)[0:65, 0:128]
                    if n > 0:
                        pv = (itk - 1) % NVB
                        mm(psO, vbl[pv][:, 0:65], p_[:, 0, :], start=True, stop=False,
                           r=["vbl%d" % pv, pkey], w=[okey])
                    mm(psO, vbl[vs][:, 0:65], p_[:, 1, :], start=(n == 0), stop=True,
                       r=["vbl%d" % vs, pkey], w=[okey])

                def emit_acc(k, itk):
                    g, dil, r_, n = its[k]
                    av = views[g][3]
                    obi = OBANKS[itk % 4]
                    okey = "ps%d" % obi
                    psO = bk(obi)[0:65, 0:128]
                    if g == 0:
                        cp("dve", av[:, r_, n, :], psO, r=[okey], w=["acc"])
                    else:
                        tt("dve", av[:, r_, n, :], psO, av[:, r_, n, :], ALU.add, r=[okey, "acc"], w=["acc"])

                nit = len(its)
                for k in range(min(LA, nit)):
                    emit_ts(k, it + k)
                for k in range(nit):
                    emit_soft(k, it + k)
                    if k + LA < nit:
                        emit_ts(k + LA, it + k + LA)
                    emit_pv(k, it + k)
                    if k >= AD:
                        emit_acc(k - AD, it + k - AD)
                for k in range(max(0, nit - AD), nit):
                    emit_acc(k, it + k)
                it += nit
                P.op("dve", lambda e: e.reciprocal(out=acc[64:65, :], in_=acc[64:65, :]), ["acc"], ["acc"])
                for cc in range(8):
                    cols = slice(cc * 512, (cc + 1) * 512)
                    bbi = 5 + cc % 2
                    bkey = "ps%d" % bbi
                    mm(bk(bbi)[0:64, :], onesf[64:65, 0:64], acc[64:65, cols], r=["onesf", "acc"], w=[bkey])
                    if b0 == 0:
                        tt("dve", obf[0:64, :], acc[0:64, cols], bk(bbi)[0:64, :], ALU.mult,
                           r=["acc", bkey], w=["obf"])
                        tt("pool", yaT[0:64, c, cols], obf[0:64, :], yaT[0:64, c, cols], ALU.mult,
                           r=["obf", "yaT"], w=["yaT"])
                    else:
                        tt("dve", obf[0:64, :], acc[0:64, cols], bk(bbi)[0:64, :], ALU.mult,
                           r=["acc", bkey], w=["obf"])
                        mm(bk(7)[64:128, :], identb[0:64, 0:64], obf[0:64, :], r=["identb", "obf"], w=["ps7"])
                        tt("dve", yaT[64:128, c, cols], bk(7)[64:128, :], yaT[64:128, c, cols], ALU.mult,
                           r=["ps7", "yaT"], w=["yaT"])

            if stop == (l, 2):
                barrier()
                dma("sp", dbg_d[3].rearrange("p (c s) -> p c s", c=2), yaT[:], w=["dbg"], key="dbg")
                break
            barrier()
            arA.reset()
            arB.reset()
            NXS = 8
            xts = [arA.take([128, D], F32) for _ in range(NXS)]
            hT3 = [arA.take([128, 8, 512], BF16) for _ in range(2)]
            yT = arA.take([128, 6, 512], BF16)
            buT = arA.take([128, 2, 512], BF16)
            gbT = arA.take([128, 2, 512], BF16)
            gcT = arA.take([128, 2, 512], BF16)
            dqT = arA.take([128, 2, 512], BF16)
            dkT = arA.take([128, 2, 512], BF16)
            hs = [arA.take([128, D], BF16) for _ in range(2)]
            gdT = arB.take([128, 2, 512], BF16)
            otmps = [arB.take([128, D], F32) for _ in range(2)]
            junk = arB.take([128, D], BF16)
            ttmp = arB.take([128, 512], F32)
            junkB = arB.take([128, 256], BF16)
            junkY = arB.take([128, D], BF16)
            poolMb = arB.take([128, 12, 128], BF16)
            sguwb = arB.take([128, 4, 128], BF16)
            biasB = arB.take([128, 2, 128], F32)
            rmf = arB.take([128, 128], F32)
            g128 = arB.take([128, 256], F32)
            poolwb = arB.take([128, 2, 64], BF16)
            vn = arB.take([128, 256], BF16)
            xcs = [arB.take([128, 256], BF16) for _ in range(2)]
            dkt = arB.take([128, 256], BF16)
            dvp = arB.take([128, 256], BF16)
            AT = arB.take([128, 4, 128], BF16)
            of_ = arB.take([128, 256], F32)
            sq = arB.take([128, 256], F32)
            onb = arB.take([128, 256], BF16)
            Sst = arB.take([128, 256], F32)
            Sbf = arB.take([128, 256], BF16)
            pooledT = arB.take([128, 2, 128], BF16)
            tmpB = arB.take([128, 2, 128], F32)
            tmpB2 = arB.take([128, 2, 128], F32)
            small = arB.take([128, 24], F32)
            st = arB.take([128, 48], F32)
            stg = otmps[0].rearrange("p (a b) -> p a b", b=128)
            stg2 = otmps[1].rearrange("p (a b) -> p a b", b=128)
            dma("sp", stg[:, 0:8, :], poolM_d[0:8].rearrange("e p n -> p e n"), w=["stg"], key="stg")
            dma("sp", stg2[:, 0:4, :], poolM_d[8:12].rearrange("e p n -> p e n"), w=["stg2a"], key="stg2a")
            dma("sp", stg2[:, 4:8, :], sguwT_d[l].rearrange("g s t -> s g t"), w=["stg2b"], key="stg2b")
            dma("sp", rmf, tri_d, w=["rmf"], key="rmf")
            cp("dve", poolMb[:, 0:8, :], stg[:, 0:8, :], r=["stg"], w=["poolMb"])
            cp("dve", poolMb[:, 8:12, :], stg2[:, 0:4, :], r=["stg2a"], w=["poolMb"])
            tt("dve", sguwb, stg2[:, 4:8, :], rmf.unsqueeze(1).to_broadcast([128, 4, 128]), ALU.mult,
               r=["stg2b", "rmf"], w=["sguwb"])
            dma("sp", rmf, rm_d, r=[], w=["rmf"], key="rmf")
            dma("sp", g128, g128_d, w=["g128"], key="g128")
            for a in range(2):
                for c2 in range(2):
                    dma("sp", biasB[64 * a:64 * a + 64, c2, :], sgub_d[l, 2 * c2 + a].partition_broadcast(64),
                        w=["biasB"], key="biasB%d%d" % (a, c2))
            pws = tmpB2[:, :, 0:64]
            for a in range(2):
                dma("sp", pws[64 * a:64 * a + 64, :, :],
                    poolw_d[l].rearrange("(c2 a) ci d -> a ci c2 d", a=2)[a],
                    w=["pws"], key="pws%d" % a)
            cp("dve", poolwb, pws, r=["pws"], w=["poolwb"])
            memset("pool", Sst, 0.0, w=["Sst"])
            memset("pool", Sbf, 0.0, w=["Sbf"])
            memset("pool", small, 0.0, w=["smallz"])
            memset("pool", st, 0.0, w=["stz"])

            sgug = vecs[:, 16 + l * 2:18 + l * 2]
            pscale = vecs[:, 20 + l * 2:22 + l * 2]
            retg = vecs[:, 24 + l * 2:26 + l * 2]
            vg4 = vecs[:, 29:33]
            xi4 = vecs[:, 33:37]

            nfm = 0

            def load_x(s_):
                for j_ in range(4):
                    t_ = 4 * s_ + j_
                    dma("sp", xts[t_ % NXS], xin_tile(t_), r=xin_keys(t_), w=["xt%d" % (t_ % NXS)],
                        key="xt%d" % (t_ % NXS))

            def p3_norm_a(s_, j_):
                t_ = 4 * s_ + j_
                norm_a(xts[t_ % NXS], "xt%d" % (t_ % NXS), hs[j_ % 2], "h%d" % (j_ % 2), junk, small, j_)

            def p3_norm_b(s_, j_):
                norm_b(hs[j_ % 2], "h%d" % (j_ % 2), hT3[s_ % 2], "hT%d" % (s_ % 2), j_, 0)

            load_x(0)
            y_carry = []
            if NORM_AHEAD:
                for j in range(4):
                    p3_norm_a(0, j)
                    p3_norm_b(0, j)
            for s in range(NST):
                hT = hT3[s % 2]
                hTkey = "hT%d" % (s % 2)
                if not NORM_AHEAD:
                    for j in range(4):
                        p3_norm_a(s, j)
                        p3_norm_b(s, j)
                P.rec = rec_f = []
                for si, kind in enumerate(("bu", "bg", "cg", "dq", "dk", "dg")):
                    for c in range(2):
                        bi = 1 + nfm % 2
                        nfm += 1
                        pk = "ps%d" % bi
                        col0 = (4 + si) * 256 + c * 128
                        for k in range(8):
                            mm(bk(bi), win_bf[:, k, col0:col0 + 128], hT[:, k, :], start=(k == 0), stop=(k == 7),
                               r=["win_bf", hTkey], w=[pk])
                        if kind == "bu":
                            cp("act", buT[:, c, :], bk(bi), r=[pk], w=["buT"])
                        elif kind == "dq":
                            cp("dve", dqT[:, c, :], bk(bi), r=[pk], w=["dqT"])
                        elif kind == "dk":
                            cp("act", dkT[:, c, :], bk(bi), r=[pk], w=["dkT"])
                        elif kind == "bg":
                            gate_evac(bk(bi), pk, ttmp, gbT[:, c, :], "gbT")
                        elif kind == "cg":
                            gate_evac(bk(bi), pk, ttmp, gcT[:, c, :], "gcT")
                        else:
                            gate_evac(bk(bi), pk, ttmp, gdT[:, c, :], "gdT")
                P.rec = None
                P.replay(interleave([rec_f, y_carry]))
                y_carry = []
                if s + 1 < NST:
                    load_x(s + 1)
                y_prev = []
                for j in range(4):
                    if p3level < 1:
                        break
                    t = 4 * s + j
                    J = slice(j * 128, (j + 1) * 128)
                    xs = xts[t % NXS]
                    xkey = "xt%d" % (t % NXS)
                    if NORM_AHEAD and s + 1 < NST:
                        p3_norm_a(s + 1, j)
                    for wbi, c0 in ((3, 10 * 256), (4, 12 * 256)):
                        for k in range(8):
                            mm(bk(wbi), hT[:, k, J], win_bf[:, k, c0:c0 + 512], start=(k == 0), stop=(k == 7),
                               r=[hTkey, "win_bf"], w=["ps%d" % wbi])
                    P.rec = rec_b = []
                    bv = bk(3)[:, 0:256]
                    red(st[:, 0:1], bv, r=["ps3", "stz"], w=["st0"])
                    act(junkB, bv, AF.Square, r=["ps3", "stz"], w=["junkB", "st1"], accum=st[:, 1:2])
                    ts(st[:, 2:3], st[:, 0:1], 1.0 / 256, None, ALU.mult, r=["st0"], w=["st2"])
                    tt("dve", st[:, 3:4], st[:, 2:3], st[:, 2:3], ALU.mult, r=["st2"], w=["st3"])
                    stt(st[:, 4:5], st[:, 1:2], 1.0 / 256, st[:, 3:4], ALU.mult, ALU.subtract,
                        r=["st1", "st3"], w=["st4"])
                    ts(st[:, 5:6], st[:, 4:5], EPS, None, ALU.add, r=["st4"], w=["st5"])
                    tt("pool", st[:, 6:7], st[:, 5:6], m05, ALU.pow, r=["st5", "vecs"], w=["st6"])
                    ts(vn, bv, st[:, 2:3], st[:, 6:7], ALU.subtract, ALU.mult, r=["ps3", "st2", "st6"], w=["vn"])
                    psBm = bk(5)[:, 0:256].rearrange("p (c t) -> p c t", t=128)
                    for g in range(4):
                        a0 = 64 * (g % 2)
                        mm(psBm[a0:a0 + 64, g // 2, :], vn[:, 64 * g:64 * g + 64], sguwb[:, g, :],
                           r=["vn", "sguwb"], w=["ps5"])
                    for c in range(2):
                        stt(tmpB[:, c, :], psBm[:, c, :], sgug[:, c:c + 1], biasB[:, c, :], ALU.mult, ALU.add,
                            r=["ps5", "vecs", "biasB"], w=["tmpB"])
                    tt("pool", tmpB2, tmpB, buT[:, :, J], ALU.mult, r=["tmpB", "buT"], w=["tmpB2"])
                    tt("pool", yT[:, 0:2, J], tmpB2, gbT[:, :, J], ALU.mult, r=["tmpB2", "gbT"], w=["yT%d" % j])
                    P.rec = rec_c = []
                    if p3level >= 2:
                        xc = xcs[t % 2]
                        xck = "xc%d" % (t % 2)
                        xcp = xcs[(t - 1) % 2]
                        xcpk = "xc%d" % ((t - 1) % 2)
                        cp("act", xc, bk(3)[:, 256:512], r=["ps3"], w=[xck])
                        psC = bk(5)[:, 256:512].rearrange("p (c t) -> p c t", t=128)
                        for g in range(4):
                            a0 = 64 * (g % 2)
                            o_ = psC[a0:a0 + 64, g // 2, :]
                            if t == 0:
                                mm(o_, xc[:, 64 * g:64 * g + 64], poolMb[:, 8 + g, :], r=[xck, "poolMb"], w=["ps5"])
                            else:
                                mm(o_, xc[:, 64 * g:64 * g + 64], poolMb[:, g, :], start=True, stop=False,
                                   r=[xck, "poolMb"], w=["ps5"])
                                mm(o_, xcp[:, 64 * g:64 * g + 64], poolMb[:, 4 + g, :], start=False, stop=True,
                                   r=[xcpk, "poolMb"], w=["ps5"])
                        cp("act", pooledT, psC, r=["ps5"], w=["pooledT"])
                        for g in range(4):
                            a0 = 64 * (g % 2)
                            mm(psC[a0:a0 + 64, g // 2, :], poolwb[a0:a0 + 64, g // 2, :], pooledT[a0:a0 + 64, g // 2, :],
                               r=["poolwb", "pooledT"], w=["ps5"])
                        for c in range(2):
                            stt(yT[:, 2 + c, J], psC[:, c, :], pscale[:, c:c + 1], gcT[:, c, J], ALU.mult, ALU.mult,
                                r=["ps5", "vecs", "gcT"], w=["yT%d" % j])
                    P.rec = rec_d = []
                    if p3level >= 3:
                        cp("act", dkt, bk(4)[:, 0:256], r=["ps4"], w=["dkt"])
                        tt("dve", dvp.rearrange("p (h e) -> p h e", e=64),
                           bk(4)[:, 256:512].rearrange("p (h e) -> p h e", e=64),
                           vg4.unsqueeze(2).to_broadcast([128, 4, 64]), ALU.mult, r=["ps4", "vecs"], w=["dvp"])
                        psDsE = bk(6)[:, 0:256].rearrange("p (c i) -> p c i", i=128)
                        psDsO = bk(7)[:, 256:512].rearrange("p (c i) -> p c i", i=128)
                        ATv = AT.rearrange("p (c a) i -> p a c i", a=2)
                        for h in range(4):
                            a0 = 64 * (h % 2)
                            dst = psDsE if h % 2 == 0 else psDsO
                            mm(dst[:, h // 2, :], dkT[a0:a0 + 64, h // 2, J], dqT[a0:a0 + 64, h // 2, J],
                               r=["dkT", "dqT"], w=["ps6" if h % 2 == 0 else "ps7"])
                        tt("dve", ATv[:, 0], psDsE, rmf.unsqueeze(1).to_broadcast([128, 2, 128]), ALU.mult,
                           r=["ps6", "rmf"], w=["AT"])
                        tt("dve", ATv[:, 1], psDsO, rmf.unsqueeze(1).to_broadcast([128, 2, 128]), ALU.mult,
                           r=["ps7", "rmf"], w=["AT"])
                        psDo = bk(7)[:, 0:256]
                        for h in range(4):
                            a0 = 64 * (h % 2)
                            mm(psDo[:, 64 * h:64 * h + 64], AT[:, h, :], dvp[:, 64 * h:64 * h + 64],
                               start=True, stop=False, r=["AT", "dvp"], w=["ps7"])
                            mm(psDo[:, 64 * h:64 * h + 64], dqT[a0:a0 + 64, h // 2, J],
                               Sbf[a0:a0 + 64, (h // 2) * 128 + a0:(h // 2) * 128 + a0 + 64],
                               start=False, stop=True, r=["dqT", "Sbf"], w=["ps7"])
                        psKv = bk(6)[:, 256:512]
                        for c in range(2):
                            mm(psKv[:, c * 128:(c + 1) * 128], dkt[:, c * 128:(c + 1) * 128], dvp[:, c * 128:(c + 1) * 128],
                               r=["dkt", "dvp"], w=["ps6"])
                        ofv = of_.rearrange("p (h e) -> p h e", e=64)
                        tt("dve", ofv, psDo.rearrange("p (h e) -> p h e", e=64),
                           xi4.unsqueeze(2).to_broadcast([128, 4, 64]), ALU.mult, r=["ps7", "vecs"], w=["of"])
                        stt(Sst, psKv, 0.125, Sst, ALU.mult, ALU.add, r=["ps6", "Sst"], w=["Sst"])
                        tt("pool", Sst, Sst, g128, ALU.mult, r=["Sst", "g128"], w=["Sst"])
                        cp("pool", Sbf, Sst, r=["Sst"], w=["Sbf"])
                        red(st[:, 8:12], ofv, r=["of"], w=["st8"])
                        tt("pool", sq, of_, of_, ALU.mult, r=["of"], w=["sq"])
                        red(st[:, 12:16], sq.rearrange("p (h e) -> p h e", e=64), r=["sq"], w=["st12"])
                        ts(st[:, 16:20], st[:, 8:12], 1.0 / 64, None, ALU.mult, r=["st8"], w=["st16"])
                        tt("dve", st[:, 20:24], st[:, 16:20], st[:, 16:20], ALU.mult, r=["st16"], w=["st20"])
                        stt(st[:, 24:28], st[:, 12:16], 1.0 / 64, st[:, 20:24], ALU.mult, ALU.subtract,
                            r=["st12", "st20"], w=["st24"])
                        ts(st[:, 28:32], st[:, 24:28], EPS, None, ALU.add, r=["st24"], w=["st28"])
                        tt("pool", st[:, 32:36], st[:, 28:32], m05.to_broadcast([128, 4]), ALU.pow,
                           r=["st28", "vecs"], w=["st32"])
                        tt("dve", ofv, ofv, st[:, 16:20].unsqueeze(2).to_broadcast([128, 4, 64]), ALU.subtract,
                           r=["of", "st16"], w=["of"])
                        tt("dve", onb.rearrange("p (h e) -> p h e", e=64), ofv,
                           st[:, 32:36].unsqueeze(2).to_broadcast([128, 4, 64]), ALU.mult, r=["of", "st32"], w=["onb"])
                        psDt = bkbf(7)[:, 0:256].rearrange("p (c t) -> p c t", t=128)
                        for c in range(2):
                            tr(psDt[:, c, :], onb[:, c * 128:(c + 1) * 128], identb[:], r=["onb", "identb"], w=["ps7"])
                        for c in range(2):
                            stt(yT[:, 4 + c, J], psDt[:, c, :], retg[:, c:c + 1], gdT[:, c, J], ALU.mult, ALU.mult,
                                r=["ps7", "vecs", "gdT"], w=["yT%d" % j])
                    P.rec = rec_n = []
                    if NORM_AHEAD and s + 1 < NST:
                        p3_norm_b(s + 1, j)
                    P.rec = rec_y = []
                    if p3level >= 4:
                        yb = 3 if j == 3 else 1
                        for n_ in range(2):
                            wbi = yb + n_
                            for kc in range(8):
                                if kc < 2:
                                    lhs = yaT[:, kc, s * 512 + j * 128:s * 512 + (j + 1) * 128]
                                    rk = "yaT"
                                else:
                                    lhs = yT[:, kc - 2, J]
                                    rk = "yT%d" % j
                                mm(bk(wbi), lhs, wout_bf[:, kc, n_ * 512:(n_ + 1) * 512], start=(kc == 0), stop=(kc == 7),
                                   r=[rk, "wout_bf"], w=["ps%d" % wbi])
                        act(junkY[:, 0:512], bk(yb), AF.Square, r=["ps%d" % yb, "stz"], w=["junkY0", "st40"], accum=st[:, 40:41])
                        act(junkY[:, 512:1024], bk(yb + 1), AF.Square, r=["ps%d" % (yb + 1), "stz"], w=["junkY1", "st41"], accum=st[:, 41:42])
                        tt("dve", st[:, 42:43], st[:, 40:41], st[:, 41:42], ALU.add, r=["st40", "st41"], w=["st42"])
                        ts(st[:, 43:44], st[:, 42:43], 1.0 / D, EPS, ALU.mult, ALU.add, r=["st42"], w=["st43"])
                        tt("pool", st[:, 44:45], st[:, 43:44], m05, ALU.pow, r=["st43", "vecs"], w=["st44"])
                        ot = otmps[t % 2]
                        otk = "otmp%d" % (t % 2)
                        for n_ in range(2):
                            stt(ot[:, n_ * 512:(n_ + 1) * 512], bk(yb + n_), st[:, 44:45], postg[:, n_ * 512:(n_ + 1) * 512],
                                ALU.mult, ALU.mult, r=["ps%d" % (yb + n_), "st44", "postg"], w=[otk])
                        tt("dve", ot, ot, xs, ALU.add, r=[otk, xkey], w=[otk])
                        dma("sp", out_t[t], ot, r=[otk], w=["od%d" % t], key=otk)
                    P.rec = None
                    P.replay(interleave([rec_b, rec_c, rec_d, rec_n, y_prev]))
                    y_prev = rec_y
                y_carry = y_prev
            P.replay(y_carry)

        final_keys = ["od%d" % t for t in range(NT)] + ["dbg"]
        names = P.finalize(final_wait_keys=final_keys)
        sems = {n: es.enter_context(nc.semaphore(n)) for n in names}
        P.emit(nc, sems)
    return nc


def _constants():
    c = {}
    c["ident"] = np.eye(128, dtype=np.float32)
    s_ = np.arange(128)[:, None]
    t_ = np.arange(128)[None, :]
    c["triT"] = (s_ <= t_).astype(np.float32)
    pm = np.zeros((12, 128, 128), np.float64)
    for g, p in enumerate(POOLS):
        cur = ((s_ >= t_ - p + 1) & (s_ <= t_)).astype(np.float64) / p - (s_ == t_)
        prev = ((s_ >= t_ - p + 129) & (s_ <= 127)).astype(np.float64) / p
        cnt = np.minimum(t_ + 1, p).astype(np.float64)
        first = ((s_ >= np.maximum(t_ - p + 1, 0)) & (s_ <= t_)).astype(np.float64) / cnt - (s_ == t_)
        pm[g], pm[4 + g], pm[8 + g] = cur, prev, first
    c["poolM"] = pm.astype(np.float32)
    et = np.zeros((12, 128, 2, 128), np.float64)
    k_ = np.arange(128)[:, None].astype(np.float64)
    q_ = np.arange(128)[None, :].astype(np.float64)
    for h in range(4):
        slope = 2.0 ** (-8.0 * (h + 1) / 4)
        for g, dil in enumerate(DILS):
            dprev = q_ + 128 - k_
            dcur = q_ - k_
            et[h * 3 + g, :, 0, :] = np.where(dprev <= 128, np.exp(-slope * dprev * dil), 0.0)
            et[h * 3 + g, :, 1, :] = np.where(dcur >= 0, np.exp(-slope * np.maximum(dcur, 0) * dil), 0.0)
    c["etab"] = et.reshape(12, 128, 256).astype(np.float32)
    c["rm"] = ((t_ >= s_).astype(np.float32) * 0.125).astype(np.float32)
    log_g = np.log(1.0 - np.exp2(-5.0 - np.arange(4, dtype=np.float64)))
    g128 = np.zeros((128, 2, 128), np.float64)
    for c2 in range(2):
        g128[0:64, c2, :] = np.exp(log_g[2 * c2] * 128)
        g128[64:128, c2, :] = np.exp(log_g[2 * c2 + 1] * 128)
    c["g128"] = g128.reshape(128, 256).astype(np.float32)
    p_ = np.arange(128, dtype=np.float64)[:, None]
    c["vg4"] = np.exp(-log_g[None, :] * (p_ + 1)).astype(np.float32)
    c["xi4"] = np.exp(log_g[None, :] * (p_ + 1)).astype(np.float32)
    return c


_CACHE = {}


def kernel(x, pre_g, w_in, sgu_g, sgu_w, sgu_b, pool_w, pool_scale, ret_g, w_out, post_g):
    f = lambda a: np.ascontiguousarray(np.asarray(a, dtype=np.float32))
    x, pre_g, w_in, sgu_g, sgu_w, sgu_b = f(x), f(pre_g), f(w_in), f(sgu_g), f(sgu_w), f(sgu_b)
    pool_w, pool_scale, ret_g, w_out, post_g = f(pool_w), f(pool_scale), f(ret_g), f(w_out), f(post_g)
    c = _constants()
    vecs = np.zeros((128, NV), np.float32)
    vecs[:, 0:16] = pre_g.reshape(DEPTH, 8, 128).transpose(2, 0, 1).reshape(128, 16)
    vecs[:, 16:20] = sgu_g.reshape(DEPTH, 2, 128).transpose(2, 0, 1).reshape(128, 4)
    vecs[:, 20:24] = pool_scale.reshape(DEPTH, 2, 128).transpose(2, 0, 1).reshape(128, 4)
    vecs[:, 24:28] = ret_g.reshape(DEPTH, 2, 128).transpose(2, 0, 1).reshape(128, 4)
    vecs[:, 28] = -0.5
    vecs[:, 29:33] = c["vg4"]
    vecs[:, 33:37] = c["xi4"]
    shared = {
        "w_in": w_in, "w_out": w_out,
        "sgu_wT": np.ascontiguousarray(sgu_w.transpose(0, 1, 3, 2)),
        "sgu_b": sgu_b, "pool_w": pool_w, "post_g": post_g, "vecs": vecs,
        "ident": c["ident"], "triT": c["triT"], "poolM": c["poolM"], "etab": c["etab"],
        "rm": c["rm"], "g128": c["g128"],
    }
    if "nc" not in _CACHE:
        _CACHE["nc"] = build_program()
    nc = _CACHE["nc"]
    n = x.shape[0]
    in_maps = [dict(shared, x=np.ascontiguousarray(x[i])) for i in range(n)]
    res = run_bass_kernel_spmd(nc, in_maps, core_ids=list(range(n)))
    return np.stack([np.asarray(r["out"], dtype=np.float32) for r in res.results], axis=0)
```

```python
import math
from contextlib import ExitStack

import numpy as np
import concourse.bass as bass
import concourse.mybir as mybir
from concourse.bass_utils import run_bass_kernel_spmd

F32 = mybir.dt.float32
BF16 = mybir.dt.bfloat16
U8 = mybir.dt.uint8
AF = mybir.ActivationFunctionType
ALU = mybir.AluOpType
AX = mybir.AxisListType

D = 1024
S = 4096
NT = S // 128
NST = S // 512
DEPTH = 2
EPS = 1e-6
DILS = (1, 4, 16)
POOLS = (2, 4, 8, 16)
COLMAP = [0, 1, 2, 3, 4, 6, 8, 9, 10, 12, 5, 7, 10, 11]
NCOL = len(COLMAP) * 256
NV = 37
SEM_LIMIT = 30000
NORM_AHEAD = True


class Op:
    __slots__ = ("eng", "fn", "reads", "writes", "dma", "deps", "idx", "ev", "ndma", "needs_inc")


class Prog:
    ENGS = ("pe", "act", "dve", "pool", "sp")

    def __init__(self):
        self.ops = []
        self.last_w = {}
        self.readers = {}
        self.last_on_eng = {}
        self.open_dmas = []
        self.barrier_idx = None
        self.rec = None

    def replay(self, items):
        assert self.rec is None
        for a in items:
            self.op(*a)

    def op(self, eng, fn, reads=(), writes=(), dma=None, ndma=1):
        if self.rec is not None:
            self.rec.append((eng, fn, tuple(reads), tuple(writes), dma, ndma))
            return None
        o = Op()
        writes = tuple(writes) + tuple(r for r in reads if r.startswith("ps"))
        reads = tuple(r for r in reads if not r.startswith("ps"))
        o.eng, o.fn, o.reads, o.writes, o.dma, o.ndma = eng, fn, tuple(reads), tuple(writes), dma, ndma
        o.ev = None
        o.needs_inc = False
        deps = set()
        for r in o.reads:
            w = self.last_w.get(r)
            if w is not None:
                deps.add(w)
        for w_ in o.writes:
            w = self.last_w.get(w_)
            if w is not None:
                deps.add(w)
            deps.update(self.readers.get(w_, ()))
        if self.barrier_idx is not None:
            deps.add(self.barrier_idx)
        o.idx = len(self.ops)
        o.deps = deps
        for r in o.reads:
            self.readers.setdefault(r, []).append(o.idx)
        for w_ in o.writes:
            self.last_w[w_] = o.idx
            self.readers[w_] = []
        self.ops.append(o)
        if dma is None:
            self.last_on_eng[eng] = o.idx
        else:
            self.open_dmas.append(o.idx)
        return o

    def barrier(self, fn):
        o = self.op("dve", fn)
        o.deps = set(self.last_on_eng.values()) | set(self.open_dmas)
        o.deps.discard(o.idx)
        self.open_dmas = []
        self.barrier_idx = o.idx
        self.last_w = {}
        self.readers = {}
        return o

    def finalize(self, final_wait_keys=()):
        ops = self.ops
        for o in ops:
            for d in o.deps:
                src = ops[d]
                if src.dma is None and src.eng == "pe" and o.eng == "pe" and o.dma is None:
                    continue
                src.needs_inc = True
        finals = [o for o in ops if o.dma is not None and any(w in final_wait_keys for w in o.writes)]
        eng_cnt = {e: 0 for e in self.ENGS}
        eng_epoch = {e: 0 for e in self.ENGS}
        dma_cnt = {}
        sem_names = set()
        for o in ops:
            if o.dma is not None:
                sname = "d_%s" % (o.dma,)
                dma_cnt[sname] = dma_cnt.get(sname, 0) + 16 * o.ndma
                o.ev = (sname, dma_cnt[sname])
                sem_names.add(sname)
            elif o.needs_inc:
                if eng_cnt[o.eng] >= SEM_LIMIT:
                    eng_epoch[o.eng] += 1
                    eng_cnt[o.eng] = 0
                eng_cnt[o.eng] += 1
                sname = "e_%s_%d" % (o.eng, eng_epoch[o.eng])
                o.ev = (sname, eng_cnt[o.eng])
                sem_names.add(sname)
        self.finals = finals
        return sorted(sem_names)

    def emit(self, nc, sems):
        ops = self.ops
        per_eng = {e: [] for e in self.ENGS}
        for o in ops:
            per_eng[o.eng].append(o)
        finals = self.finals

        def run_stream(engname, eng):
            waited = {}
            for o in per_eng[engname]:
                need = {}
                for d in o.deps:
                    src = ops[d]
                    if src.ev is None:
                        continue
                    if src.dma is None and src.eng == "pe" and engname == "pe" and o.dma is None:
                        continue
                    s, v = src.ev
                    if need.get(s, 0) < v:
                        need[s] = v
                for s, v in need.items():
                    if waited.get(s, 0) >= v:
                        continue
                    eng.wait_ge(sems[s], v)
                    waited[s] = v
                if o.dma is not None:
                    insts = o.fn(eng)
                    if not isinstance(insts, (list, tuple)):
                        insts = [insts]
                    assert len(insts) == o.ndma, (len(insts), o.ndma)
                    for ins in insts:
                        ins.then_inc(sems[o.ev[0]], 16)
                else:
                    ins = o.fn(eng)
                    if o.needs_inc:
                        ins.then_inc(sems[o.ev[0]], 1)
            if engname == "sp":
                need = {}
                for o in finals:
                    s, v = o.ev
                    if need.get(s, 0) < v:
                        need[s] = v
                for s, v in need.items():
                    eng.wait_ge(sems[s], v)

        with nc.Block() as block:
            @block.sync
            def _(e):
                run_stream("sp", e)

            @block.tensor
            def _(e):
                run_stream("pe", e)

            @block.scalar
            def _(e):
                run_stream("act", e)

            @block.vector
            def _(e):
                run_stream("dve", e)

            @block.gpsimd
            def _(e):
                run_stream("pool", e)


def interleave(lists):
    items = []
    for li, L in enumerate(lists):
        n = len(L)
        for m, a in enumerate(L):
            items.append(((m + 0.5) / n, li, m, a))
    items.sort(key=lambda z: (z[0], z[1], z[2]))
    return [z[3] for z in items]


class Arena:
    def __init__(self, ap_u8, nbytes):
        self.ap = ap_u8
        self.n = nbytes
        self.off = 0

    def reset(self):
        self.off = 0

    def take(self, shape, dt):
        esz = 4 if dt == F32 else 2
        n = 1
        for s_ in shape[1:]:
            n *= s_
        nb = (n * esz + 31) // 32 * 32
        assert self.off + nb <= self.n, ("arena overflow", self.off, nb, self.n)
        v = self.ap[:, self.off:self.off + n * esz].bitcast(dt)
        self.off += nb
        if len(shape) == 3:
            v = v.rearrange("p (a b) -> p a b", b=shape[2])
        elif len(shape) == 4:
            v = v.rearrange("p (a b c) -> p a b c", b=shape[2], c=shape[3])
        return v


def build_program(depth=DEPTH, stop=None, p3level=4):
    nc = bass.Bass("TRN2", target_bir_lowering=False)

    def din(name, shape):
        return nc.dram_tensor(name, list(shape), F32, kind="ExternalInput").ap()

    x_d = din("x", [S, D])
    win_d = din("w_in", [DEPTH, D, 13 * 256])
    wout_d = din("w_out", [DEPTH, D, D])
    sguwT_d = din("sgu_wT", [DEPTH, 4, 128, 128])
    sgub_d = din("sgu_b", [DEPTH, 4, 128])
    poolw_d = din("pool_w", [DEPTH, 4, 64, 64])
    postg_d = din("post_g", [DEPTH, D])
    vecs_d = din("vecs", [128, NV])
    ident_d = din("ident", [128, 128])
    tri_d = din("triT", [128, 128])
    poolM_d = din("poolM", [12, 128, 128])
    etab_d = din("etab", [12, 128, 256])
    rm_d = din("rm", [128, 128])
    g128_d = din("g128", [128, 256])
    out_d = nc.dram_tensor("out", [S, D], F32, kind="ExternalOutput").ap()
    if stop is not None:
        dbg_d = nc.dram_tensor("dbg", [4, 128, 2 * S], BF16, kind="ExternalOutput").ap()

    x_t = x_d.rearrange("(t p) d -> t p d", p=128)
    out_t = out_d.rearrange("(t p) d -> t p d", p=128)

    es = ExitStack()
    with es:
        def sb(name, shape, dt):
            return es.enter_context(nc.sbuf_tensor(name, shape, dt))

        win_bf = sb("win_bf", [128, 8, NCOL], BF16)
        wout_bf = sb("wout_bf", [128, 8, D], BF16)
        yaT = sb("yaT", [128, 2, S], BF16)
        identb = sb("identb", [128, 128], BF16)
        identf = sb("identf", [128, 128], F32)
        vecs = sb("vecs_sb", [128, NV], F32)
        postg = sb("postg", [128, D], F32)
        scr = sb("scr", [128, 8], F32)
        A_BYTES = 69632
        B_BYTES = 45056
        arA_t = sb("arenaA", [128, A_BYTES], U8)
        arB_t = sb("arenaB", [128, B_BYTES], U8)
        arA = Arena(arA_t, A_BYTES)
        arB = Arena(arB_t, B_BYTES)
        banks = [es.enter_context(nc.psum_tensor("bank%d" % i, [128, 512], F32)) for i in range(8)]

        def bk(i):
            return banks[i][:, :]

        def bkbf(i):
            return banks[i][:, :].bitcast(BF16)

        P = Prog()

        def mm(out, lhsT, rhs, start=True, stop=True, r=(), w=()):
            P.op("pe", lambda e: e.matmul(out, lhsT=lhsT, rhs=rhs, start=start, stop=stop), r, w)

        def tr(out, in_, ident, r=(), w=()):
            P.op("pe", lambda e: e.transpose(out=out, in_=in_, identity=ident), r, w)

        def act(out, in_, func, r=(), w=(), scale=1.0, accum=None):
            if accum is None:
                P.op("act", lambda e: e.activation(out=out, in_=in_, func=func, scale=scale), r, w)
            else:
                P.op("act", lambda e: e.activation(out=out, in_=in_, func=func, scale=scale, accum_out=accum), r, w)

        def tt(eng, out, in0, in1, op, r=(), w=()):
            P.op(eng, lambda e: e.tensor_tensor(out=out, in0=in0, in1=in1, op=op), r, w)

        def ts(out, in0, s1, s2, op0, op1=None, r=(), w=()):
            if op1 is None:
                P.op("dve", lambda e: e.tensor_scalar(out=out, in0=in0, scalar1=s1, scalar2=None, op0=op0), r, w)
            else:
                P.op("dve", lambda e: e.tensor_scalar(out=out, in0=in0, scalar1=s1, scalar2=s2, op0=op0, op1=op1), r, w)

        def stt(out, in0, scalar, in1, op0, op1, r=(), w=()):
            P.op("dve", lambda e: e.scalar_tensor_tensor(out=out, in0=in0, scalar=scalar, in1=in1, op0=op0, op1=op1), r, w)

        def cp(eng, out, in_, r=(), w=()):
            if eng == "act":
                P.op("act", lambda e: e.activation(out=out, in_=in_, func=AF.Copy), r, w)
            else:
                P.op(eng, lambda e: e.tensor_copy(out=out, in_=in_), r, w)

        def red(out, in_, r=(), w=()):
            P.op("dve", lambda e: e.tensor_reduce(out=out, in_=in_, axis=AX.X, op=ALU.add), r, w)

        def dma(eng, out, in_, r=(), w=(), key=None):
            P.op(eng, lambda e: e.dma_start(out=out, in_=in_), r, w, dma=key)

        def memset(eng, ap, val, w=()):
            P.op(eng, lambda e: e.memset(ap, val), (), w)

        def barrier():
            P.barrier(lambda e: e.memset(scr[:, 0:1], 0.0))

        m05 = vecs[:, 28:29]

        dma("sp", identf[:], ident_d, w=["identf"], key="identf")
        dma("sp", vecs[:], vecs_d, w=["vecs"], key="vecs")
        cp("dve", identb[:], identf[:], r=["identf"], w=["identb"])

        rr = {"i": 0}

        def alt(engs):
            rr["i"] += 1
            return engs[rr["i"] % len(engs)]

        def norm_a(xs, xkey, h_ap, hkey, junk, small, col):
            ss = small[:, col:col + 1]
            tmp = small[:, 8 + col:9 + col]
            rstd = small[:, 16 + col:17 + col]
            skey = "small%d" % col
            act(junk, xs, AF.Square, r=[xkey, "smallz"], w=["junk", skey], accum=ss)
            ts(tmp, ss, 1.0 / D, EPS, ALU.mult, ALU.add, r=[skey], w=[skey + "t"])
            tt("pool", rstd, tmp, m05, ALU.pow, r=[skey + "t", "vecs"], w=[skey + "r"])
            ts(h_ap, xs, rstd, None, ALU.mult, r=[xkey, skey + "r"], w=[hkey])

        def norm_b(h_ap, hkey, hT, hTkey, j, tbank):
            pT = bkbf(tbank).rearrange("p (k t) -> p k t", t=128)
            for k in range(8):
                tr(pT[:, k, :], h_ap[:, k * 128:(k + 1) * 128], identb[:], r=[hkey, "identb"], w=["ps%d" % tbank])
            cp("act", hT[:, :, j * 128:(j + 1) * 128], pT, r=["ps%d" % tbank], w=[hTkey])

        def norm_transpose(xs, xkey, h_ap, hkey, junk, small, col, hT, hTkey, j, tbank):
            norm_a(xs, xkey, h_ap, hkey, junk, small, col)
            norm_b(h_ap, hkey, hT, hTkey, j, tbank)

        def gate_evac(ps_ap, pskey, ttmp, out_ap, outkey):
            act(ttmp, ps_ap, AF.Tanh, r=[pskey], w=["ttmp"], scale=0.5)
            stt(out_ap, ttmp, 1.0, ps_ap, ALU.add, ALU.mult, r=["ttmp", pskey], w=[outkey])

        for l in range(depth):
            def xin_tile(t):
                return x_t[t] if l == 0 else out_t[t]

            def xin_keys(t):
                return [] if l == 0 else ["od%d" % t]

            barrier()
            arA.reset()
            arB.reset()
            stages = [arA.take([128, 8, 256], F32) for _ in range(4)]
            pregb = vecs[:, l * 8:(l + 1) * 8].unsqueeze(2).to_broadcast([128, 8, 256])
            nload = 0
            for i, src in enumerate(COLMAP):
                st = stages[nload % 4]
                sk = "stage%d" % (nload % 4)
                nload += 1
                dma("sp", st, win_d[l, :, src * 256:(src + 1) * 256].rearrange("(k p) n -> p k n", p=128),
                    w=[sk], key=sk)
                tt(alt(["dve", "pool"]), win_bf[:, :, i * 256:(i + 1) * 256], st, pregb, ALU.mult,
                   r=[sk, "vecs"], w=["win_bf"])
            for i in range(4):
                st = stages[nload % 4]
                sk = "stage%d" % (nload % 4)
                nload += 1
                dma("sp", st, wout_d[l, :, i * 256:(i + 1) * 256].rearrange("(k p) n -> p k n", p=128),
                    w=[sk], key=sk)
                act(wout_bf[:, :, i * 256:(i + 1) * 256], st, AF.Copy, r=[sk], w=["wout_bf"], scale=0.5)
            dma("sp", postg[:], postg_d[l].partition_broadcast(128), w=["postg"], key="postg")

            barrier()
            arA.reset()
            arB.reset()
            qT = arA.take([128, 2, S], BF16)
            kT = arA.take([128, 2, S], BF16)
            vT = arA.take([128, 2, S], BF16)
            xts = [arB.take([128, D], F32) for _ in range(4)]
            hs = [arB.take([128, D], BF16) for _ in range(3)]
            hTs = [arB.take([128, 8, 512], BF16) for _ in range(2)]
            junk = arB.take([128, D], BF16)
            ttmp = arB.take([128, 512], F32)
            small = arB.take([128, 24], F32)
            memset("pool", small, 0.0, w=["smallz"])
            nchunk = 0

            p1 = {"a": 0}

            def p1_ensure_a(upto):
                while p1["a"] <= upto and p1["a"] < NT:
                    t_ = p1["a"]
                    xkey_ = "xt%d" % (t_ % 4)
                    dma("sp", xts[t_ % 4], xin_tile(t_), r=xin_keys(t_), w=[xkey_], key=xkey_)
                    norm_a(xts[t_ % 4], xkey_, hs[t_ % 3], "h%d" % (t_ % 3), junk, small, t_ % 4)
                    p1["a"] += 1

            def p1_b(t_):
                p1_ensure_a(t_ + 2)
                norm_b(hs[t_ % 3], "h%d" % (t_ % 3), hTs[(t_ // 4) % 2], "hT%d" % ((t_ // 4) % 2), t_ % 4, t_ % 2)

            p1_ensure_a(1)
            for j in range(4):
                p1_b(j)
            for s in range(NST):
                hT = hTs[s % 2]
                hTkey = "hT%d" % (s % 2)
                tok = slice(s * 512, (s + 1) * 512)
                nloc = 0
                for sl in range(4):
                    for c in range(2):
                        bi = 2 + nchunk % 3
                        nchunk += 1
                        pk = "ps%d" % bi
                        col0 = sl * 256 + c * 128
                        for k in range(8):
                            mm(bk(bi), win_bf[:, k, col0:col0 + 128], hT[:, k, :], start=(k == 0), stop=(k == 7),
                               r=["win_bf", hTkey], w=[pk])
                        if sl < 3:
                            dst = (qT, kT, vT)[sl]
                            cp(alt(["act", "dve"]), dst[:, c, tok], bk(bi), r=[pk], w=["qkv"])
                        else:
                            gate_evac(bk(bi), pk, ttmp, yaT[:, c, tok], "yaT")
                        nloc += 1
                        if s + 1 < NST and nloc % 2 == 0:
                            p1_b(4 * (s + 1) + nloc // 2 - 1)

            if stop == (l, 1):
                barrier()
                for i_, t_ in enumerate((qT, kT, vT, yaT)):
                    dma("sp", dbg_d[i_].rearrange("p (c s) -> p c s", c=2), t_ if i_ < 3 else t_[:],
                        w=["dbg"], key="dbg")
                break
            barrier()
            arB.reset()
            etab = arB.take([128, 12, 256], F32)
            acc = arB.take([128, S], F32)
            NSB = 4
            NVB = 6
            SBANKS = (1, 2, 5)
            OBANKS = (3, 4, 6, 7)
            LA = 2
            AD = 2
            ess = [arB.take([128, 2, 128], F32) for _ in range(NSB)]
            pTs = [arB.take([128, 2, 128], BF16) for _ in range(NSB)]
            vbl = [arB.take([128, 80], BF16) for _ in range(NVB)]
            onesf = arB.take([128, 64], F32)
            obf = arB.take([128, 512], BF16)
            dma("sp", etab, etab_d.rearrange("e p n -> p e n"), w=["etab"], key="etab")
            memset("pool", onesf, 1.0, w=["onesf"])
            for i in range(NVB):
                memset("pool", vbl[i], 1.0, w=["vbl%d" % i])
            it = 0
            for h in range(4):
                c = h // 2
                b0 = 64 * (h % 2)
                qh = qT[b0:b0 + 64, c, :]
                kh = kT[b0:b0 + 64, c, :]
                vh = vT[b0:b0 + 64, c, :]
                idh = identb[b0:b0 + 64, b0:b0 + 64]
                its = []
                for g, dil in enumerate(DILS):
                    nb = NT // dil
                    for r_ in range(dil):
                        for n in range(nb):
                            its.append((g, dil, r_, n))
                views = {}
                for g, dil in enumerate(DILS):
                    views[g] = (
                        qh.rearrange("p (n i r) -> p r n i", r=dil, i=128),
                        kh.rearrange("p (n i r) -> p r n i", r=dil, i=128),
                        vh.rearrange("p (n i r) -> p r n i", r=dil, i=128),
                        acc[0:65, :].rearrange("p (n i r) -> p r n i", r=dil, i=128),
                        etab[:, h * 3 + g, :].rearrange("p (a q) -> p a q", q=128),
                    )

                def emit_ts(k, itk):
                    g, dil, r_, n = its[k]
                    qv, kv_, vv, av, et = views[g]
                    vs = itk % NVB
                    psV = bkbf(0)[:, 0:64]
                    tr(psV, vv[:, r_, n, :], idh, r=["qkv", "identb"], w=["ps0"])
                    cp("act", vbl[vs][:, 0:64], psV, r=["ps0"], w=["vbl%d" % vs])
                    sbi = SBANKS[itk % 3]
                    skey = "ps%d" % sbi
                    psS = bk(sbi)[:, 0:256].rearrange("p (a q) -> p a q", q=128)
                    if n > 0:
                        mm(psS[:, 0, :], kv_[:, r_, n - 1, :], qv[:, r_, n, :], r=["qkv"], w=[skey])
                    mm(psS[:, 1, :], kv_[:, r_, n, :], qv[:, r_, n, :], r=["qkv"], w=[skey])

                def emit_soft(k, itk):
                    g, dil, r_, n = its[k]
                    qv, kv_, vv, av, et = views[g]
                    sbi = SBANKS[itk % 3]
                    skey = "ps%d" % sbi
                    psS = bk(sbi)[:, 0:256].rearrange("p (a q) -> p a q", q=128)
                    lo = 0 if n > 0 else 1
                    e_ = ess[itk % NSB]
                    p_ = pTs[itk % NSB]
                    act(e_[:, lo:2, :], psS[:, lo:2, :], AF.Exp, r=[skey], w=["es%d" % (itk % NSB)], scale=0.125)
                    tt("dve", p_[:, lo:2, :], e_[:, lo:2, :], et[:, lo:2, :], ALU.mult,
                       r=["es%d" % (itk % NSB), "etab"], w=["pT%d" % (itk % NSB)])

                def emit_pv(k, itk):
                    g, dil, r_, n = its[k]
                    p_ = pTs[itk % NSB]
                    pkey = "pT%d" % (itk % NSB)
                    vs = itk % NVB
                    obi = OBANKS[itk % 4]
                    okey = "ps%d" % obi
                    psO = bk(obi)[0:65, 0:128]
                    if n > 0:
                        pv = (itk - 1) % NVB
                        mm(psO, vbl[pv][:, 0:65], p_[:, 0, :], start=True, stop=False,
                           r=["vbl%d" % pv, pkey], w=[okey])
                    mm(psO, vbl[vs][:, 0:65], p_[:, 1, :], start=(n == 0), stop=True,
                       r=["vbl%d" % vs, pkey], w=[okey])

                def emit_acc(k, itk):
                    g, dil, r_, n = its[k]
                    av = views[g][3]
                    obi = OBANKS[itk % 4]
                    okey = "ps%d" % obi
                    psO = bk(obi)[0:65, 0:128]
                    if g == 0:
                        cp("dve", av[:, r_, n, :], psO, r=[okey], w=["acc"])
                    else:
                        tt("dve", av[:, r_, n, :], psO, av[:, r_, n, :], ALU.add, r=[okey, "acc"], w=["acc"])

                nit = len(its)
                for k in range(min(LA, nit)):
                    emit_ts(k, it + k)
                for k in range(nit):
                    emit_soft(k, it + k)
                    if k + LA < nit:
                        emit_ts(k + LA, it + k + LA)
                    emit_pv(k, it + k)
                    if k >= AD:
                        emit_acc(k - AD, it + k - AD)
                for k in range(max(0, nit - AD), nit):
                    emit_acc(k, it + k)
                it += nit
                P.op("dve", lambda e: e.reciprocal(out=acc[64:65, :], in_=acc[64:65, :]), ["acc"], ["acc"])
                for cc in range(8):
                    cols = slice(cc * 512, (cc + 1) * 512)
                    bbi = 5 + cc % 2
                    bkey = "ps%d" % bbi
                    mm(bk(bbi)[0:64, :], onesf[64:65, 0:64], acc[64:65, cols], r=["onesf", "acc"], w=[bkey])
                    if b0 == 0:
                        tt("dve", obf[0:64, :], acc[0:64, cols], bk(bbi)[0:64, :], ALU.mult,
                           r=["acc", bkey], w=["obf"])
                        tt("pool", yaT[0:64, c, cols], obf[0:64, :], yaT[0:64, c, cols], ALU.mult,
                           r=["obf", "yaT"], w=["yaT"])
                    else:
                        tt("dve", obf[0:64, :], acc[0:64, cols], bk(bbi)[0:64, :], ALU.mult,
                           r=["acc", bkey], w=["obf"])
                        mm(bk(7)[64:128, :], identb[0:64, 0:64], obf[0:64, :], r=["identb", "obf"], w=["ps7"])
                        tt("dve", yaT[64:128, c, cols], bk(7)[64:128, :], yaT[64:128, c, cols], ALU.mult,
                           r=["ps7", "yaT"], w=["yaT"])

            if stop == (l, 2):
                barrier()
                dma("sp", dbg_d[3].rearrange("p (c s) -> p c s", c=2), yaT[:], w=["dbg"], key="dbg")
                break
            barrier()
            arA.reset()
            arB.reset()
            NXS = 8
            xts = [arA.take([128, D], F32) for _ in range(NXS)]
            hT3 = [arA.take([128, 8, 512], BF16) for _ in range(2)]
            yT = arA.take([128, 6, 512], BF16)
            buT = arA.take([128, 2, 512], BF16)
            gbT = arA.take([128, 2, 512], BF16)
            gcT = arA.take([128, 2, 512], BF16)
            dqT = arA.take([128, 2, 512], BF16)
            dkT = arA.take([128, 2, 512], BF16)
            hs = [arA.take([128, D], BF16) for _ in range(2)]
            gdT = arB.take([128, 2, 512], BF16)
            otmps = [arB.take([128, D], F32) for _ in range(2)]
            junk = arB.take([128, D], BF16)
            ttmp = arB.take([128, 512], F32)
            junkB = arB.take([128, 256], BF16)
            junkY = arB.take([128, D], BF16)
            poolMb = arB.take([128, 12, 128], BF16)
            sguwb = arB.take([128, 4, 128], BF16)
            biasB = arB.take([128, 2, 128], F32)
            rmf = arB.take([128, 128], F32)
            g128 = arB.take([128, 256], F32)
            poolwb = arB.take([128, 2, 64], BF16)
            vn = arB.take([128, 256], BF16)
            xcs = [arB.take([128, 256], BF16) for _ in range(2)]
            dkt = arB.take([128, 256], BF16)
            dvp = arB.take([128, 256], BF16)
            AT = arB.take([128, 4, 128], BF16)
            of_ = arB.take([128, 256], F32)
            sq = arB.take([128, 256], F32)
            onb = arB.take([128, 256], BF16)
            Sst = arB.take([128, 256], F32)
            Sbf = arB.take([128, 256], BF16)
            pooledT = arB.take([128, 2, 128], BF16)
            tmpB = arB.take([128, 2, 128], F32)
            tmpB2 = arB.take([128, 2, 128], F32)
            small = arB.take([128, 24], F32)
            st = arB.take([128, 48], F32)
            stg = otmps[0].rearrange("p (a b) -> p a b", b=128)
            stg2 = otmps[1].rearrange("p (a b) -> p a b", b=128)
            dma("sp", stg[:, 0:8, :], poolM_d[0:8].rearrange("e p n -> p e n"), w=["stg"], key="stg")
            dma("sp", stg2[:, 0:4, :], poolM_d[8:12].rearrange("e p n -> p e n"), w=["stg2a"], key="stg2a")
            dma("sp", stg2[:, 4:8, :], sguwT_d[l].rearrange("g s t -> s g t"), w=["stg2b"], key="stg2b")
            dma("sp", rmf, tri_d, w=["rmf"], key="rmf")
            cp("dve", poolMb[:, 0:8, :], stg[:, 0:8, :], r=["stg"], w=["poolMb"])
            cp("dve", poolMb[:, 8:12, :], stg2[:, 0:4, :], r=["stg2a"], w=["poolMb"])
            tt("dve", sguwb, stg2[:, 4:8, :], rmf.unsqueeze(1).to_broadcast([128, 4, 128]), ALU.mult,
               r=["stg2b", "rmf"], w=["sguwb"])
            dma("sp", rmf, rm_d, r=[], w=["rmf"], key="rmf")
            dma("sp", g128, g128_d, w=["g128"], key="g128")
            for a in range(2):
                for c2 in range(2):
                    dma("sp", biasB[64 * a:64 * a + 64, c2, :], sgub_d[l, 2 * c2 + a].partition_broadcast(64),
                        w=["biasB"], key="biasB%d%d" % (a, c2))
            pws = tmpB2[:, :, 0:64]
            for a in range(2):
                dma("sp", pws[64 * a:64 * a + 64, :, :],
                    poolw_d[l].rearrange("(c2 a) ci d -> a ci c2 d", a=2)[a],
                    w=["pws"], key="pws%d" % a)
            cp("dve", poolwb, pws, r=["pws"], w=["poolwb"])
            memset("pool", Sst, 0.0, w=["Sst"])
            memset("pool", Sbf, 0.0, w=["Sbf"])
            memset("pool", small, 0.0, w=["smallz"])
            memset("pool", st, 0.0, w=["stz"])

            sgug = vecs[:, 16 + l * 2:18 + l * 2]
            pscale = vecs[:, 20 + l * 2:22 + l * 2]
            retg = vecs[:, 24 + l * 2:26 + l * 2]
            vg4 = vecs[:, 29:33]
            xi4 = vecs[:, 33:37]

            nfm = 0

            def load_x(s_):
                for j_ in range(4):
                    t_ = 4 * s_ + j_
                    dma("sp", xts[t_ % NXS], xin_tile(t_), r=xin_keys(t_), w=["xt%d" % (t_ % NXS)],
                        key="xt%d" % (t_ % NXS))

            def p3_norm_a(s_, j_):
                t_ = 4 * s_ + j_
                norm_a(xts[t_ % NXS], "xt%d" % (t_ % NXS), hs[j_ % 2], "h%d" % (j_ % 2), junk, small, j_)

            def p3_norm_b(s_, j_):
                norm_b(hs[j_ % 2], "h%d" % (j_ % 2), hT3[s_ % 2], "hT%d" % (s_ % 2), j_, 0)

            load_x(0)
            y_carry = []
            if NORM_AHEAD:
                for j in range(4):
                    p3_norm_a(0, j)
                    p3_norm_b(0, j)
            for s in range(NST):
                hT = hT3[s % 2]
                hTkey = "hT%d" % (s % 2)
                if not NORM_AHEAD:
                    for j in range(4):
                        p3_norm_a(s, j)
                        p3_norm_b(s, j)
                P.rec = rec_f = []
                for si, kind in enumerate(("bu", "bg", "cg", "dq", "dk", "dg")):
                    for c in range(2):
                        bi = 1 + nfm % 2
                        nfm += 1
                        pk = "ps%d" % bi
                        col0 = (4 + si) * 256 + c * 128
                        for k in range(8):
                            mm(bk(bi), win_bf[:, k, col0:col0 + 128], hT[:, k, :], start=(k == 0), stop=(k == 7),
                               r=["win_bf", hTkey], w=[pk])
                        if kind == "bu":
                            cp("act", buT[:, c, :], bk(bi), r=[pk], w=["buT"])
                        elif kind == "dq":
                            cp("dve", dqT[:, c, :], bk(bi), r=[pk], w=["dqT"])
                        elif kind == "dk":
                            cp("act", dkT[:, c, :], bk(bi), r=[pk], w=["dkT"])
                        elif kind == "bg":
                            gate_evac(bk(bi), pk, ttmp, gbT[:, c, :], "gbT")
                        elif kind == "cg":
                            gate_evac(bk(bi), pk, ttmp, gcT[:, c, :], "gcT")
                        else:
                            gate_evac(bk(bi), pk, ttmp, gdT[:, c, :], "gdT")
                P.rec = None
                P.replay(interleave([rec_f, y_carry]))
                y_carry = []
                if s + 1 < NST:
                    load_x(s + 1)
                y_prev = []
                for j in range(4):
                    if p3level < 1:
                        break
                    t = 4 * s + j
                    J = slice(j * 128, (j + 1) * 128)
                    xs = xts[t % NXS]
                    xkey = "xt%d" % (t % NXS)
                    if NORM_AHEAD and s + 1 < NST:
                        p3_norm_a(s + 1, j)
                    for wbi, c0 in ((3, 10 * 256), (4, 12 * 256)):
                        for k in range(8):
                            mm(bk(wbi), hT[:, k, J], win_bf[:, k, c0:c0 + 512], start=(k == 0), stop=(k == 7),
                               r=[hTkey, "win_bf"], w=["ps%d" % wbi])
                    P.rec = rec_b = []
                    bv = bk(3)[:, 0:256]
                    red(st[:, 0:1], bv, r=["ps3", "stz"], w=["st0"])
                    act(junkB, bv, AF.Square, r=["ps3", "stz"], w=["junkB", "st1"], accum=st[:, 1:2])
                    ts(st[:, 2:3], st[:, 0:1], 1.0 / 256, None, ALU.mult, r=["st0"], w=["st2"])
                    tt("dve", st[:, 3:4], st[:, 2:3], st[:, 2:3], ALU.mult, r=["st2"], w=["st3"])
                    stt(st[:, 4:5], st[:, 1:2], 1.0 / 256, st[:, 3:4], ALU.mult, ALU.subtract,
                        r=["st1", "st3"], w=["st4"])
                    ts(st[:, 5:6], st[:, 4:5], EPS, None, ALU.add, r=["st4"], w=["st5"])
                    tt("pool", st[:, 6:7], st[:, 5:6], m05, ALU.pow, r=["st5", "vecs"], w=["st6"])
                    ts(vn, bv, st[:, 2:3], st[:, 6:7], ALU.subtract, ALU.mult, r=["ps3", "st2", "st6"], w=["vn"])
                    psBm = bk(5)[:, 0:256].rearrange("p (c t) -> p c t", t=128)
                    for g in range(4):
                        a0 = 64 * (g % 2)
                        mm(psBm[a0:a0 + 64, g // 2, :], vn[:, 64 * g:64 * g + 64], sguwb[:, g, :],
                           r=["vn", "sguwb"], w=["ps5"])
                    for c in range(2):
                        stt(tmpB[:, c, :], psBm[:, c, :], sgug[:, c:c + 1], biasB[:, c, :], ALU.mult, ALU.add,
                            r=["ps5", "vecs", "biasB"], w=["tmpB"])
                    tt("dve", tmpB2, tmpB, buT[:, :, J], ALU.mult, r=["tmpB", "buT"], w=["tmpB2"])
                    tt("dve", yT[:, 0:2, J], tmpB2, gbT[:, :, J], ALU.mult, r=["tmpB2", "gbT"], w=["yT%d" % j])
                    P.rec = rec_c = []
                    if p3level >= 2:
                        xc = xcs[t % 2]
                        xck = "xc%d" % (t % 2)
                        xcp = xcs[(t - 1) % 2]
                        xcpk = "xc%d" % ((t - 1) % 2)
                        cp("act", xc, bk(3)[:, 256:512], r=["ps3"], w=[xck])
                        psC = bk(5)[:, 256:512].rearrange("p (c t) -> p c t", t=128)
                        for g in range(4):
                            a0 = 64 * (g % 2)
                            o_ = psC[a0:a0 + 64, g // 2, :]
                            if t == 0:
                                mm(o_, xc[:, 64 * g:64 * g + 64], poolMb[:, 8 + g, :], r=[xck, "poolMb"], w=["ps5"])
                            else:
                                mm(o_, xc[:, 64 * g:64 * g + 64], poolMb[:, g, :], start=True, stop=False,
                                   r=[xck, "poolMb"], w=["ps5"])
                                mm(o_, xcp[:, 64 * g:64 * g + 64], poolMb[:, 4 + g, :], start=False, stop=True,
                                   r=[xcpk, "poolMb"], w=["ps5"])
                        cp("act", pooledT, psC, r=["ps5"], w=["pooledT"])
                        for g in range(4):
                            a0 = 64 * (g % 2)
                            mm(psC[a0:a0 + 64, g // 2, :], poolwb[a0:a0 + 64, g // 2, :], pooledT[a0:a0 + 64, g // 2, :],
                               r=["poolwb", "pooledT"], w=["ps5"])
                        for c in range(2):
                            stt(yT[:, 2 + c, J], psC[:, c, :], pscale[:, c:c + 1], gcT[:, c, J], ALU.mult, ALU.mult,
                                r=["ps5", "vecs", "gcT"], w=["yT%d" % j])
                    P.rec = rec_d = []
                    if p3level >= 3:
                        cp("act", dkt, bk(4)[:, 0:256], r=["ps4"], w=["dkt"])
                        tt("dve", dvp.rearrange("p (h e) -> p h e", e=64),
                           bk(4)[:, 256:512].rearrange("p (h e) -> p h e", e=64),
                           vg4.unsqueeze(2).to_broadcast([128, 4, 64]), ALU.mult, r=["ps4", "vecs"], w=["dvp"])
                        psDsE = bk(6)[:, 0:256].rearrange("p (c i) -> p c i", i=128)
                        psDsO = bk(7)[:, 256:512].rearrange("p (c i) -> p c i", i=128)
                        ATv = AT.rearrange("p (c a) i -> p a c i", a=2)
                        for h in range(4):
                            a0 = 64 * (h % 2)
                            dst = psDsE if h % 2 == 0 else psDsO
                            mm(dst[:, h // 2, :], dkT[a0:a0 + 64, h // 2, J], dqT[a0:a0 + 64, h // 2, J],
                               r=["dkT", "dqT"], w=["ps6" if h % 2 == 0 else "ps7"])
                        tt("dve", ATv[:, 0], psDsE, rmf.unsqueeze(1).to_broadcast([128, 2, 128]), ALU.mult,
                           r=["ps6", "rmf"], w=["AT"])
                        tt("dve", ATv[:, 1], psDsO, rmf.unsqueeze(1).to_broadcast([128, 2, 128]), ALU.mult,
                           r=["ps7", "rmf"], w=["AT"])
                        psDo = bk(7)[:, 0:256]
                        for h in range(4):
                            a0 = 64 * (h % 2)
                            mm(psDo[:, 64 * h:64 * h + 64], AT[:, h, :], dvp[:, 64 * h:64 * h + 64],
                               start=True, stop=False, r=["AT", "dvp"], w=["ps7"])
                            mm(psDo[:, 64 * h:64 * h + 64], dqT[a0:a0 + 64, h // 2, J],
                               Sbf[a0:a0 + 64, (h // 2) * 128 + a0:(h // 2) * 128 + a0 + 64],
                               start=False, stop=True, r=["dqT", "Sbf"], w=["ps7"])
                        psKv = bk(6)[:, 256:512]
                        for c in range(2):
                            mm(psKv[:, c * 128:(c + 1) * 128], dkt[:, c * 128:(c + 1) * 128], dvp[:, c * 128:(c + 1) * 128],
                               r=["dkt", "dvp"], w=["ps6"])
                        ofv = of_.rearrange("p (h e) -> p h e", e=64)
                        tt("dve", ofv, psDo.rearrange("p (h e) -> p h e", e=64),
                           xi4.unsqueeze(2).to_broadcast([128, 4, 64]), ALU.mult, r=["ps7", "vecs"], w=["of"])
                        stt(Sst, psKv, 0.125, Sst, ALU.mult, ALU.add, r=["ps6", "Sst"], w=["Sst"])
                        tt("dve", Sst, Sst, g128, ALU.mult, r=["Sst", "g128"], w=["Sst"])
                        cp("act", Sbf, Sst, r=["Sst"], w=["Sbf"])
                        red(st[:, 8:12], ofv, r=["of"], w=["st8"])
                        act(sq, of_, AF.Square, r=["of"], w=["sq"])
                        red(st[:, 12:16], sq.rearrange("p (h e) -> p h e", e=64), r=["sq"], w=["st12"])
                        ts(st[:, 16:20], st[:, 8:12], 1.0 / 64, None, ALU.mult, r=["st8"], w=["st16"])
                        tt("dve", st[:, 20:24], st[:, 16:20], st[:, 16:20], ALU.mult, r=["st16"], w=["st20"])
                        stt(st[:, 24:28], st[:, 12:16], 1.0 / 64, st[:, 20:24], ALU.mult, ALU.subtract,
                            r=["st12", "st20"], w=["st24"])
                        ts(st[:, 28:32], st[:, 24:28], EPS, None, ALU.add, r=["st24"], w=["st28"])
                        tt("pool", st[:, 32:36], st[:, 28:32], m05.to_broadcast([128, 4]), ALU.pow,
                           r=["st28", "vecs"], w=["st32"])
                        tt("dve", ofv, ofv, st[:, 16:20].unsqueeze(2).to_broadcast([128, 4, 64]), ALU.subtract,
                           r=["of", "st16"], w=["of"])
                        tt("dve", onb.rearrange("p (h e) -> p h e", e=64), ofv,
                           st[:, 32:36].unsqueeze(2).to_broadcast([128, 4, 64]), ALU.mult, r=["of", "st32"], w=["onb"])
                        psDt = bkbf(7)[:, 0:256].rearrange("p (c t) -> p c t", t=128)
                        for c in range(2):
                            tr(psDt[:, c, :], onb[:, c * 128:(c + 1) * 128], identb[:], r=["onb", "identb"], w=["ps7"])
                        for c in range(2):
                            stt(yT[:, 4 + c, J], psDt[:, c, :], retg[:, c:c + 1], gdT[:, c, J], ALU.mult, ALU.mult,
                                r=["ps7", "vecs", "gdT"], w=["yT%d" % j])
                    P.rec = rec_n = []
                    if NORM_AHEAD and s + 1 < NST:
                        p3_norm_b(s + 1, j)
                    P.rec = rec_y = []
                    if p3level >= 4:
                        yb = 3 if j == 3 else 1
                        for n_ in range(2):
                            wbi = yb + n_
                            for kc in range(8):
                                if kc < 2:
                                    lhs = yaT[:, kc, s * 512 + j * 128:s * 512 + (j + 1) * 128]
                                    rk = "yaT"
                                else:
                                    lhs = yT[:, kc - 2, J]
                                    rk = "yT%d" % j
                                mm(bk(wbi), lhs, wout_bf[:, kc, n_ * 512:(n_ + 1) * 512], start=(kc == 0), stop=(kc == 7),
                                   r=[rk, "wout_bf"], w=["ps%d" % wbi])
                        act(junkY[:, 0:512], bk(yb), AF.Square, r=["ps%d" % yb, "stz"], w=["junkY0", "st40"], accum=st[:, 40:41])
                        act(junkY[:, 512:1024], bk(yb + 1), AF.Square, r=["ps%d" % (yb + 1), "stz"], w=["junkY1", "st41"], accum=st[:, 41:42])
                        tt("dve", st[:, 42:43], st[:, 40:41], st[:, 41:42], ALU.add, r=["st40", "st41"], w=["st42"])
                        ts(st[:, 43:44], st[:, 42:43], 1.0 / D, EPS, ALU.mult, ALU.add, r=["st42"], w=["st43"])
                        tt("pool", st[:, 44:45], st[:, 43:44], m05, ALU.pow, r=["st43", "vecs"], w=["st44"])
                        ot = otmps[t % 2]
                        otk = "otmp%d" % (t % 2)
                        for n_ in range(2):
                            stt(ot[:, n_ * 512:(n_ + 1) * 512], bk(yb + n_), st[:, 44:45], postg[:, n_ * 512:(n_ + 1) * 512],
                                ALU.mult, ALU.mult, r=["ps%d" % (yb + n_), "st44", "postg"], w=[otk])
                        tt("dve", ot, ot, xs, ALU.add, r=[otk, xkey], w=[otk])
                        dma("sp", out_t[t], ot, r=[otk], w=["od%d" % t], key=otk)
                    P.rec = None
                    P.replay(interleave([rec_b, rec_c, rec_d, rec_n, y_prev]))
                    y_prev = rec_y
                y_carry = y_prev
            P.replay(y_carry)

        final_keys = ["od%d" % t for t in range(NT)] + ["dbg"]
        names = P.finalize(final_wait_keys=final_keys)
        sems = {n: es.enter_context(nc.semaphore(n)) for n in names}
        P.emit(nc, sems)
    return nc


def _constants():
    c = {}
    c["ident"] = np.eye(128, dtype=np.float32)
    s_ = np.arange(128)[:, None]
    t_ = np.arange(128)[None, :]
    c["triT"] = (s_ <= t_).astype(np.float32)
    pm = np.zeros((12, 128, 128), np.float64)
    for g, p in enumerate(POOLS):
        cur = ((s_ >= t_ - p + 1) & (s_ <= t_)).astype(np.float64) / p - (s_ == t_)
        prev = ((s_ >= t_ - p + 129) & (s_ <= 127)).astype(np.float64) / p
        cnt = np.minimum(t_ + 1, p).astype(np.float64)
        first = ((s_ >= np.maximum(t_ - p + 1, 0)) & (s_ <= t_)).astype(np.float64) / cnt - (s_ == t_)
        pm[g], pm[4 + g], pm[8 + g] = cur, prev, first
    c["poolM"] = pm.astype(np.float32)
    et = np.zeros((12, 128, 2, 128), np.float64)
    k_ = np.arange(128)[:, None].astype(np.float64)
    q_ = np.arange(128)[None, :].astype(np.float64)
    for h in range(4):
        slope = 2.0 ** (-8.0 * (h + 1) / 4)
        for g, dil in enumerate(DILS):
            dprev = q_ + 128 - k_
            dcur = q_ - k_
            et[h * 3 + g, :, 0, :] = np.where(dprev <= 128, np.exp(-slope * dprev * dil), 0.0)
            et[h * 3 + g, :, 1, :] = np.where(dcur >= 0, np.exp(-slope * np.maximum(dcur, 0) * dil), 0.0)
    c["etab"] = et.reshape(12, 128, 256).astype(np.float32)
    c["rm"] = ((t_ >= s_).astype(np.float32) * 0.125).astype(np.float32)
    log_g = np.log(1.0 - np.exp2(-5.0 - np.arange(4, dtype=np.float64)))
    g128 = np.zeros((128, 2, 128), np.float64)
    for c2 in range(2):
        g128[0:64, c2, :] = np.exp(log_g[2 * c2] * 128)
        g128[64:128, c2, :] = np.exp(log_g[2 * c2 + 1] * 128)
    c["g128"] = g128.reshape(128, 256).astype(np.float32)
    p_ = np.arange(128, dtype=np.float64)[:, None]
    c["vg4"] = np.exp(-log_g[None, :] * (p_ + 1)).astype(np.float32)
    c["xi4"] = np.exp(log_g[None, :] * (p_ + 1)).astype(np.float32)
    return c


_CACHE = {}


def kernel(x, pre_g, w_in, sgu_g, sgu_w, sgu_b, pool_w, pool_scale, ret_g, w_out, post_g):
    f = lambda a: np.ascontiguousarray(np.asarray(a, dtype=np.float32))
    x, pre_g, w_in, sgu_g, sgu_w, sgu_b = f(x), f(pre_g), f(w_in), f(sgu_g), f(sgu_w), f(sgu_b)
    pool_w, pool_scale, ret_g, w_out, post_g = f(pool_w), f(pool_scale), f(ret_g), f(w_out), f(post_g)
    c = _constants()
    vecs = np.zeros((128, NV), np.float32)
    vecs[:, 0:16] = pre_g.reshape(DEPTH, 8, 128).transpose(2, 0, 1).reshape(128, 16)
    vecs[:, 16:20] = sgu_g.reshape(DEPTH, 2, 128).transpose(2, 0, 1).reshape(128, 4)
    vecs[:, 20:24] = pool_scale.reshape(DEPTH, 2, 128).transpose(2, 0, 1).reshape(128, 4)
    vecs[:, 24:28] = ret_g.reshape(DEPTH, 2, 128).transpose(2, 0, 1).reshape(128, 4)
    vecs[:, 28] = -0.5
    vecs[:, 29:33] = c["vg4"]
    vecs[:, 33:37] = c["xi4"]
    shared = {
        "w_in": w_in, "w_out": w_out,
        "sgu_wT": np.ascontiguousarray(sgu_w.transpose(0, 1, 3, 2)),
        "sgu_b": sgu_b, "pool_w": pool_w, "post_g": post_g, "vecs": vecs,
        "ident": c["ident"], "triT": c["triT"], "poolM": c["poolM"], "etab": c["etab"],
        "rm": c["rm"], "g128": c["g128"],
    }
    if "nc" not in _CACHE:
        _CACHE["nc"] = build_program()
    nc = _CACHE["nc"]
    n = x.shape[0]
    in_maps = [dict(shared, x=np.ascontiguousarray(x[i])) for i in range(n)]
    res = run_bass_kernel_spmd(nc, in_maps, core_ids=list(range(n)))
    return np.stack([np.asarray(r["out"], dtype=np.float32) for r in res.results], axis=0)
```

```python
import math
from contextlib import ExitStack

import numpy as np
import concourse.bass as bass
import concourse.mybir as mybir
from concourse.bass_utils import run_bass_kernel_spmd

F32 = mybir.dt.float32
BF16 = mybir.dt.bfloat16
U8 = mybir.dt.uint8
AF = mybir.ActivationFunctionType
ALU = mybir.AluOpType
AX = mybir.AxisListType

D = 1024
S = 4096
NT = S // 128
NST = S // 512
DEPTH = 2
EPS = 1e-6
DILS = (1, 4, 16)
POOLS = (2, 4, 8, 16)
COLMAP = [0, 1, 2, 3, 4, 6, 8, 9, 10, 12, 5, 7, 10, 11]
NCOL = len(COLMAP) * 256
NV = 37
SEM_LIMIT = 30000
NORM_AHEAD = True


class Op:
    __slots__ = ("eng", "fn", "reads", "writes", "dma", "deps", "idx", "ev", "ndma", "needs_inc")


class Prog:
    ENGS = ("pe", "act", "dve", "pool", "sp")

    def __init__(self):
        self.ops = []
        self.last_w = {}
        self.readers = {}
        self.last_on_eng = {}
        self.open_dmas = []
        self.barrier_idx = None
        self.rec = None

    def replay(self, items):
        assert self.rec is None
        for a in items:
            self.op(*a)

    def op(self, eng, fn, reads=(), writes=(), dma=None, ndma=1):
        if self.rec is not None:
            self.rec.append((eng, fn, tuple(reads), tuple(writes), dma, ndma))
            return None
        o = Op()
        writes = tuple(writes) + tuple(r for r in reads if r.startswith("ps"))
        reads = tuple(r for r in reads if not r.startswith("ps"))
        o.eng, o.fn, o.reads, o.writes, o.dma, o.ndma = eng, fn, tuple(reads), tuple(writes), dma, ndma
        o.ev = None
        o.needs_inc = False
        deps = set()
        for r in o.reads:
            w = self.last_w.get(r)
            if w is not None:
                deps.add(w)
        for w_ in o.writes:
            w = self.last_w.get(w_)
            if w is not None:
                deps.add(w)
            deps.update(self.readers.get(w_, ()))
        if self.barrier_idx is not None:
            deps.add(self.barrier_idx)
        o.idx = len(self.ops)
        o.deps = deps
        for r in o.reads:
            self.readers.setdefault(r, []).append(o.idx)
        for w_ in o.writes:
            self.last_w[w_] = o.idx
            self.readers[w_] = []
        self.ops.append(o)
        if dma is None:
            self.last_on_eng[eng] = o.idx
        else:
            self.open_dmas.append(o.idx)
        return o

    def barrier(self, fn):
        o = self.op("dve", fn)
        o.deps = set(self.last_on_eng.values()) | set(self.open_dmas)
        o.deps.discard(o.idx)
        self.open_dmas = []
        self.barrier_idx = o.idx
        self.last_w = {}
        self.readers = {}
        return o

    def finalize(self, final_wait_keys=()):
        ops = self.ops
        for o in ops:
            for d in o.deps:
                src = ops[d]
                if src.dma is None and src.eng == "pe" and o.eng == "pe" and o.dma is None:
                    continue
                src.needs_inc = True
        finals = [o for o in ops if o.dma is not None and any(w in final_wait_keys for w in o.writes)]
        eng_cnt = {e: 0 for e in self.ENGS}
        eng_epoch = {e: 0 for e in self.ENGS}
        dma_cnt = {}
        sem_names = set()
        for o in ops:
            if o.dma is not None:
                sname = "d_%s" % (o.dma,)
                dma_cnt[sname] = dma_cnt.get(sname, 0) + 16 * o.ndma
                o.ev = (sname, dma_cnt[sname])
                sem_names.add(sname)
            elif o.needs_inc:
                if eng_cnt[o.eng] >= SEM_LIMIT:
                    eng_epoch[o.eng] += 1
                    eng_cnt[o.eng] = 0
                eng_cnt[o.eng] += 1
                sname = "e_%s_%d" % (o.eng, eng_epoch[o.eng])
                o.ev = (sname, eng_cnt[o.eng])
                sem_names.add(sname)
        self.finals = finals
        return sorted(sem_names)

    def emit(self, nc, sems):
        ops = self.ops
        per_eng = {e: [] for e in self.ENGS}
        for o in ops:
            per_eng[o.eng].append(o)
        finals = self.finals

        def run_stream(engname, eng):
            waited = {}
            for o in per_eng[engname]:
                need = {}
                for d in o.deps:
                    src = ops[d]
                    if src.ev is None:
                        continue
                    if src.dma is None and src.eng == "pe" and engname == "pe" and o.dma is None:
                        continue
                    s, v = src.ev
                    if need.get(s, 0) < v:
                        need[s] = v
                for s, v in need.items():
                    if waited.get(s, 0) >= v:
                        continue
                    eng.wait_ge(sems[s], v)
                    waited[s] = v
                if o.dma is not None:
                    insts = o.fn(eng)
                    if not isinstance(insts, (list, tuple)):
                        insts = [insts]
                    assert len(insts) == o.ndma, (len(insts), o.ndma)
                    for ins in insts:
                        ins.then_inc(sems[o.ev[0]], 16)
                else:
                    ins = o.fn(eng)
                    if o.needs_inc:
                        ins.then_inc(sems[o.ev[0]], 1)
            if engname == "sp":
                need = {}
                for o in finals:
                    s, v = o.ev
                    if need.get(s, 0) < v:
                        need[s] = v
                for s, v in need.items():
                    eng.wait_ge(sems[s], v)

        with nc.Block() as block:
            @block.sync
            def _(e):
                run_stream("sp", e)

            @block.tensor
            def _(e):
                run_stream("pe", e)

            @block.scalar
            def _(e):
                run_stream("act", e)

            @block.vector
            def _(e):
                run_stream("dve", e)

            @block.gpsimd
            def _(e):
                run_stream("pool", e)


def interleave(lists):
    items = []
    for li, L in enumerate(lists):
        n = len(L)
        for m, a in enumerate(L):
            items.append(((m + 0.5) / n, li, m, a))
    items.sort(key=lambda z: (z[0], z[1], z[2]))
    return [z[3] for z in items]


class Arena:
    def __init__(self, ap_u8, nbytes):
        self.ap = ap_u8
        self.n = nbytes
        self.off = 0

    def reset(self):
        self.off = 0

    def take(self, shape, dt):
        esz = 4 if dt == F32 else 2
        n = 1
        for s_ in shape[1:]:
            n *= s_
        nb = (n * esz + 31) // 32 * 32
        assert self.off + nb <= self.n, ("arena overflow", self.off, nb, self.n)
        v = self.ap[:, self.off:self.off + n * esz].bitcast(dt)
        self.off += nb
        if len(shape) == 3:
            v = v.rearrange("p (a b) -> p a b", b=shape[2])
        elif len(shape) == 4:
            v = v.rearrange("p (a b c) -> p a b c", b=shape[2], c=shape[3])
        return v


def build_program(depth=DEPTH, stop=None, p3level=4):
    nc = bass.Bass("TRN2", target_bir_lowering=False)

    def din(name, shape):
        return nc.dram_tensor(name, list(shape), F32, kind="ExternalInput").ap()

    x_d = din("x", [S, D])
    win_d = din("w_in", [DEPTH, D, 13 * 256])
    wout_d = din("w_out", [DEPTH, D, D])
    sguwT_d = din("sgu_wT", [DEPTH, 4, 128, 128])
    sgub_d = din("sgu_b", [DEPTH, 4, 128])
    poolw_d = din("pool_w", [DEPTH, 4, 64, 64])
    postg_d = din("post_g", [DEPTH, D])
    vecs_d = din("vecs", [128, NV])
    ident_d = din("ident", [128, 128])
    tri_d = din("triT", [128, 128])
    poolM_d = din("poolM", [12, 128, 128])
    etab_d = din("etab", [12, 128, 256])
    rm_d = din("rm", [128, 128])
    g128_d = din("g128", [128, 256])
    out_d = nc.dram_tensor("out", [S, D], F32, kind="ExternalOutput").ap()
    if stop is not None:
        dbg_d = nc.dram_tensor("dbg", [4, 128, 2 * S], BF16, kind="ExternalOutput").ap()

    x_t = x_d.rearrange("(t p) d -> t p d", p=128)
    out_t = out_d.rearrange("(t p) d -> t p d", p=128)

    es = ExitStack()
    with es:
        def sb(name, shape, dt):
            return es.enter_context(nc.sbuf_tensor(name, shape, dt))

        win_bf = sb("win_bf", [128, 8, NCOL], BF16)
        wout_bf = sb("wout_bf", [128, 8, D], BF16)
        yaT = sb("yaT", [128, 2, S], BF16)
        identb = sb("identb", [128, 128], BF16)
        identf = sb("identf", [128, 128], F32)
        vecs = sb("vecs_sb", [128, NV], F32)
        postg = sb("postg", [128, D], F32)
        scr = sb("scr", [128, 8], F32)
        A_BYTES = 69632
        B_BYTES = 45056
        arA_t = sb("arenaA", [128, A_BYTES], U8)
        arB_t = sb("arenaB", [128, B_BYTES], U8)
        arA = Arena(arA_t, A_BYTES)
        arB = Arena(arB_t, B_BYTES)
        banks = [es.enter_context(nc.psum_tensor("bank%d" % i, [128, 512], F32)) for i in range(8)]

        def bk(i):
            return banks[i][:, :]

        def bkbf(i):
            return banks[i][:, :].bitcast(BF16)

        P = Prog()

        def mm(out, lhsT, rhs, start=True, stop=True, r=(), w=()):
            P.op("pe", lambda e: e.matmul(out, lhsT=lhsT, rhs=rhs, start=start, stop=stop), r, w)

        def tr(out, in_, ident, r=(), w=()):
            P.op("pe", lambda e: e.transpose(out=out, in_=in_, identity=ident), r, w)

        def act(out, in_, func, r=(), w=(), scale=1.0, accum=None):
            if accum is None:
                P.op("act", lambda e: e.activation(out=out, in_=in_, func=func, scale=scale), r, w)
            else:
                P.op("act", lambda e: e.activation(out=out, in_=in_, func=func, scale=scale, accum_out=accum), r, w)

        def tt(eng, out, in0, in1, op, r=(), w=()):
            P.op(eng, lambda e: e.tensor_tensor(out=out, in0=in0, in1=in1, op=op), r, w)

        def ts(out, in0, s1, s2, op0, op1=None, r=(), w=()):
            if op1 is None:
                P.op("dve", lambda e: e.tensor_scalar(out=out, in0=in0, scalar1=s1, scalar2=None, op0=op0), r, w)
            else:
                P.op("dve", lambda e: e.tensor_scalar(out=out, in0=in0, scalar1=s1, scalar2=s2, op0=op0, op1=op1), r, w)

        def stt(out, in0, scalar, in1, op0, op1, r=(), w=()):
            P.op("dve", lambda e: e.scalar_tensor_tensor(out=out, in0=in0, scalar=scalar, in1=in1, op0=op0, op1=op1), r, w)

        def cp(eng, out, in_, r=(), w=()):
            if eng == "act":
                P.op("act", lambda e: e.activation(out=out, in_=in_, func=AF.Copy), r, w)
            else:
                P.op(eng, lambda e: e.tensor_copy(out=out, in_=in_), r, w)

        def red(out, in_, r=(), w=()):
            P.op("dve", lambda e: e.tensor_reduce(out=out, in_=in_, axis=AX.X, op=ALU.add), r, w)

        def dma(eng, out, in_, r=(), w=(), key=None):
            P.op(eng, lambda e: e.dma_start(out=out, in_=in_), r, w, dma=key)

        def memset(eng, ap, val, w=()):
            P.op(eng, lambda e: e.memset(ap, val), (), w)

        def barrier():
            P.barrier(lambda e: e.memset(scr[:, 0:1], 0.0))

        m05 = vecs[:, 28:29]

        dma("sp", identf[:], ident_d, w=["identf"], key="identf")
        dma("sp", vecs[:], vecs_d, w=["vecs"], key="vecs")
        cp("dve", identb[:], identf[:], r=["identf"], w=["identb"])

        rr = {"i": 0}

        def alt(engs):
            rr["i"] += 1
            return engs[rr["i"] % len(engs)]

        def norm_a(xs, xkey, h_ap, hkey, junk, small, col):
            ss = small[:, col:col + 1]
            tmp = small[:, 8 + col:9 + col]
            rstd = small[:, 16 + col:17 + col]
            skey = "small%d" % col
            act(junk, xs, AF.Square, r=[xkey, "smallz"], w=["junk", skey], accum=ss)
            ts(tmp, ss, 1.0 / D, EPS, ALU.mult, ALU.add, r=[skey], w=[skey + "t"])
            tt("pool", rstd, tmp, m05, ALU.pow, r=[skey + "t", "vecs"], w=[skey + "r"])
            ts(h_ap, xs, rstd, None, ALU.mult, r=[xkey, skey + "r"], w=[hkey])

        def norm_b(h_ap, hkey, hT, hTkey, j, tbank):
            pT = bkbf(tbank).rearrange("p (k t) -> p k t", t=128)
            for k in range(8):
                tr(pT[:, k, :], h_ap[:, k * 128:(k + 1) * 128], identb[:], r=[hkey, "identb"], w=["ps%d" % tbank])
            cp("act", hT[:, :, j * 128:(j + 1) * 128], pT, r=["ps%d" % tbank], w=[hTkey])

        def norm_transpose(xs, xkey, h_ap, hkey, junk, small, col, hT, hTkey, j, tbank):
            norm_a(xs, xkey, h_ap, hkey, junk, small, col)
            norm_b(h_ap, hkey, hT, hTkey, j, tbank)

        def gate_evac(ps_ap, pskey, ttmp, out_ap, outkey):
            act(ttmp, ps_ap, AF.Tanh, r=[pskey], w=["ttmp"], scale=0.5)
            stt(out_ap, ttmp, 1.0, ps_ap, ALU.add, ALU.mult, r=["ttmp", pskey], w=[outkey])

        for l in range(depth):
            def xin_tile(t):
                return x_t[t] if l == 0 else out_t[t]

            def xin_keys(t):
                return [] if l == 0 else ["od%d" % t]

            barrier()
            arA.reset()
            arB.reset()
            stages = [arA.take([128, 8, 256], F32) for _ in range(4)]
            pregb = vecs[:, l * 8:(l + 1) * 8].unsqueeze(2).to_broadcast([128, 8, 256])
            nload = 0
            for i, src in enumerate(COLMAP):
                st = stages[nload % 4]
                sk = "stage%d" % (nload % 4)
                nload += 1
                dma("sp", st, win_d[l, :, src * 256:(src + 1) * 256].rearrange("(k p) n -> p k n", p=128),
                    w=[sk], key=sk)
                tt("dve", win_bf[:, :, i * 256:(i + 1) * 256], st, pregb, ALU.mult,
                   r=[sk, "vecs"], w=["win_bf"])
            for i in range(4):
                st = stages[nload % 4]
                sk = "stage%d" % (nload % 4)
                nload += 1
                dma("sp", st, wout_d[l, :, i * 256:(i + 1) * 256].rearrange("(k p) n -> p k n", p=128),
                    w=[sk], key=sk)
                act(wout_bf[:, :, i * 256:(i + 1) * 256], st, AF.Copy, r=[sk], w=["wout_bf"], scale=0.5)
            dma("sp", postg[:], postg_d[l].partition_broadcast(128), w=["postg"], key="postg")

            barrier()
            arA.reset()
            arB.reset()
            qT = arA.take([128, 2, S], BF16)
            kT = arA.take([128, 2, S], BF16)
            vT = arA.take([128, 2, S], BF16)
            xts = [arB.take([128, D], F32) for _ in range(4)]
            hs = [arB.take([128, D], BF16) for _ in range(3)]
            hTs = [arB.take([128, 8, 512], BF16) for _ in range(2)]
            junk = arB.take([128, D], BF16)
            ttmp = arB.take([128, 512], F32)
            small = arB.take([128, 24], F32)
            memset("pool", small, 0.0, w=["smallz"])
            nchunk = 0

            p1 = {"a": 0}

            def p1_ensure_a(upto):
                while p1["a"] <= upto and p1["a"] < NT:
                    t_ = p1["a"]
                    xkey_ = "xt%d" % (t_ % 4)
                    dma("sp", xts[t_ % 4], xin_tile(t_), r=xin_keys(t_), w=[xkey_], key=xkey_)
                    norm_a(xts[t_ % 4], xkey_, hs[t_ % 3], "h%d" % (t_ % 3), junk, small, t_ % 4)
                    p1["a"] += 1

            def p1_b(t_):
                p1_ensure_a(t_ + 2)
                norm_b(hs[t_ % 3], "h%d" % (t_ % 3), hTs[(t_ // 4) % 2], "hT%d" % ((t_ // 4) % 2), t_ % 4, t_ % 2)

            p1_ensure_a(1)
            for j in range(4):
                p1_b(j)
            for s in range(NST):
                hT = hTs[s % 2]
                hTkey = "hT%d" % (s % 2)
                tok = slice(s * 512, (s + 1) * 512)
                nloc = 0
                for sl in range(4):
                    for c in range(2):
                        bi = 2 + nchunk % 3
                        nchunk += 1
                        pk = "ps%d" % bi
                        col0 = sl * 256 + c * 128
                        for k in range(8):
                            mm(bk(bi), win_bf[:, k, col0:col0 + 128], hT[:, k, :], start=(k == 0), stop=(k == 7),
                               r=["win_bf", hTkey], w=[pk])
                        if sl < 3:
                            dst = (qT, kT, vT)[sl]
                            cp(alt(["act", "dve"]), dst[:, c, tok], bk(bi), r=[pk], w=["qkv"])
                        else:
                            gate_evac(bk(bi), pk, ttmp, yaT[:, c, tok], "yaT")
                        nloc += 1
                        if s + 1 < NST and nloc % 2 == 0:
                            p1_b(4 * (s + 1) + nloc // 2 - 1)

            if stop == (l, 1):
                barrier()
                for i_, t_ in enumerate((qT, kT, vT, yaT)):
                    dma("sp", dbg_d[i_].rearrange("p (c s) -> p c s", c=2), t_ if i_ < 3 else t_[:],
                        w=["dbg"], key="dbg")
                break
            barrier()
            arB.reset()
            etab = arB.take([128, 12, 256], F32)
            acc = arB.take([128, S], F32)
            NSB = 4
            NVB = 6
            SBANKS = (1, 2, 5)
            OBANKS = (3, 4, 6, 7)
            LA = 2
            AD = 2
            ess = [arB.take([128, 2, 128], F32) for _ in range(NSB)]
            pTs = [arB.take([128, 2, 128], BF16) for _ in range(NSB)]
            vbl = [arB.take([128, 80], BF16) for _ in range(NVB)]
            onesf = arB.take([128, 64], F32)
            obf = arB.take([128, 512], BF16)
            dma("sp", etab, etab_d.rearrange("e p n -> p e n"), w=["etab"], key="etab")
            memset("pool", onesf, 1.0, w=["onesf"])
            for i in range(NVB):
                memset("pool", vbl[i], 1.0, w=["vbl%d" % i])
            it = 0
            for h in range(4):
                c = h // 2
                b0 = 64 * (h % 2)
                qh = qT[b0:b0 + 64, c, :]
                kh = kT[b0:b0 + 64, c, :]
                vh = vT[b0:b0 + 64, c, :]
                idh = identb[b0:b0 + 64, b0:b0 + 64]
                its = []
                for g, dil in enumerate(DILS):
                    nb = NT // dil
                    for r_ in range(dil):
                        for n in range(nb):
                            its.append((g, dil, r_, n))
                views = {}
                for g, dil in enumerate(DILS):
                    views[g] = (
                        qh.rearrange("p (n i r) -> p r n i", r=dil, i=128),
                        kh.rearrange("p (n i r) -> p r n i", r=dil, i=128),
                        vh.rearrange("p (n i r) -> p r n i", r=dil, i=128),
                        acc[0:65, :].rearrange("p (n i r) -> p r n i", r=dil, i=128),
                        etab[:, h * 3 + g, :].rearrange("p (a q) -> p a q", q=128),
                    )

                def emit_ts(k, itk):
                    g, dil, r_, n = its[k]
                    qv, kv_, vv, av, et = views[g]
                    vs = itk % NVB
                    psV = bkbf(0)[:, 0:64]
                    tr(psV, vv[:, r_, n, :], idh, r=["qkv", "identb"], w=["ps0"])
                    cp("act", vbl[vs][:, 0:64], psV, r=["ps0"], w=["vbl%d" % vs])
                    sbi = SBANKS[itk % 3]
                    skey = "ps%d" % sbi
                    psS = bk(sbi)[:, 0:256].rearrange("p (a q) -> p a q", q=128)
                    if n > 0:
                        mm(psS[:, 0, :], kv_[:, r_, n - 1, :], qv[:, r_, n, :], r=["qkv"], w=[skey])
                    mm(psS[:, 1, :], kv_[:, r_, n, :], qv[:, r_, n, :], r=["qkv"], w=[skey])

                def emit_soft(k, itk):
                    g, dil, r_, n = its[k]
                    qv, kv_, vv, av, et = views[g]
                    sbi = SBANKS[itk % 3]
                    skey = "ps%d" % sbi
                    psS = bk(sbi)[:, 0:256].rearrange("p (a q) -> p a q", q=128)
                    lo = 0 if n > 0 else 1
                    e_ = ess[itk % NSB]
                    p_ = pTs[itk % NSB]
                    act(e_[:, lo:2, :], psS[:, lo:2, :], AF.Exp, r=[skey], w=["es%d" % (itk % NSB)], scale=0.125)
                    tt("dve", p_[:, lo:2, :], e_[:, lo:2, :], et[:, lo:2, :], ALU.mult,
                       r=["es%d" % (itk % NSB), "etab"], w=["pT%d" % (itk % NSB)])

                def emit_pv(k, itk):
                    g, dil, r_, n = its[k]
                    p_ = pTs[itk % NSB]
                    pkey = "pT%d" % (itk % NSB)
                    vs = itk % NVB
                    obi = OBANKS[itk % 4]
                    okey = "ps%d" % obi
                    psO = bk(obi)[0:65, 0:128]
                    if n > 0:
                        pv = (itk - 1) % NVB
                        mm(psO, vbl[pv][:, 0:65], p_[:, 0, :], start=True, stop=False,
                           r=["vbl%d" % pv, pkey], w=[okey])
                    mm(psO, vbl[vs][:, 0:65], p_[:, 1, :], start=(n == 0), stop=True,
                       r=["vbl%d" % vs, pkey], w=[okey])

                def emit_acc(k, itk):
                    g, dil, r_, n = its[k]
                    av = views[g][3]
                    obi = OBANKS[itk % 4]
                    okey = "ps%d" % obi
                    psO = bk(obi)[0:65, 0:128]
                    if g == 0:
                        cp("dve", av[:, r_, n, :], psO, r=[okey], w=["acc"])
                    else:
                        tt("dve", av[:, r_, n, :], psO, av[:, r_, n, :], ALU.add, r=[okey, "acc"], w=["acc"])

                nit = len(its)
                for k in range(min(LA, nit)):
                    emit_ts(k, it + k)
                for k in range(nit):
                    emit_soft(k, it + k)
                    if k + LA < nit:
                        emit_ts(k + LA, it + k + LA)
                    emit_pv(k, it + k)
                    if k >= AD:
                        emit_acc(k - AD, it + k - AD)
                for k in range(max(0, nit - AD), nit):
                    emit_acc(k, it + k)
                it += nit
                P.op("dve", lambda e: e.reciprocal(out=acc[64:65, :], in_=acc[64:65, :]), ["acc"], ["acc"])
                for cc in range(8):
                    cols = slice(cc * 512, (cc + 1) * 512)
                    bbi = 5 + cc % 2
                    bkey = "ps%d" % bbi
                    mm(bk(bbi)[0:64, :], onesf[64:65, 0:64], acc[64:65, cols], r=["onesf", "acc"], w=[bkey])
                    if b0 == 0:
                        tt("dve", obf[0:64, :], acc[0:64, cols], bk(bbi)[0:64, :], ALU.mult,
                           r=["acc", bkey], w=["obf"])
                        tt("dve", yaT[0:64, c, cols], obf[0:64, :], yaT[0:64, c, cols], ALU.mult,
                           r=["obf", "yaT"], w=["yaT"])
                    else:
                        tt("dve", obf[0:64, :], acc[0:64, cols], bk(bbi)[0:64, :], ALU.mult,
                           r=["acc", bkey], w=["obf"])
                        mm(bk(7)[64:128, :], identb[0:64, 0:64], obf[0:64, :], r=["identb", "obf"], w=["ps7"])
                        tt("dve", yaT[64:128, c, cols], bk(7)[64:128, :], yaT[64:128, c, cols], ALU.mult,
                           r=["ps7", "yaT"], w=["yaT"])

            if stop == (l, 2):
                barrier()
                dma("sp", dbg_d[3].rearrange("p (c s) -> p c s", c=2), yaT[:], w=["dbg"], key="dbg")
                break
            barrier()
            arA.reset()
            arB.reset()
            NXS = 8
            xts = [arA.take([128, D], F32) for _ in range(NXS)]
            hT3 = [arA.take([128, 8, 512], BF16) for _ in range(2)]
            yT = arA.take([128, 6, 512], BF16)
            buT = arA.take([128, 2, 512], BF16)
            gbT = arA.take([128, 2, 512], BF16)
            gcT = arA.take([128, 2, 512], BF16)
            dqT = arA.take([128, 2, 512], BF16)
            dkT = arA.take([128, 2, 512], BF16)
            hs = [arA.take([128, D], BF16) for _ in range(2)]
            gdT = arB.take([128, 2, 512], BF16)
            otmps = [arB.take([128, D], F32) for _ in range(2)]
            junk = arB.take([128, D], BF16)
            ttmp = arB.take([128, 512], F32)
            junkB = arB.take([128, 256], BF16)
            junkY = arB.take([128, D], BF16)
            poolMb = arB.take([128, 12, 128], BF16)
            sguwb = arB.take([128, 4, 128], BF16)
            biasB = arB.take([128, 2, 128], F32)
            rmf = arB.take([128, 128], F32)
            g128 = arB.take([128, 256], F32)
            poolwb = arB.take([128, 2, 64], BF16)
            vn = arB.take([128, 256], BF16)
            xcs = [arB.take([128, 256], BF16) for _ in range(2)]
            dkt = arB.take([128, 256], BF16)
            dvp = arB.take([128, 256], BF16)
            AT = arB.take([128, 4, 128], BF16)
            of_ = arB.take([128, 256], F32)
            sq = arB.take([128, 256], F32)
            onb = arB.take([128, 256], BF16)
            Sst = arB.take([128, 256], F32)
            Sbf = arB.take([128, 256], BF16)
            pooledT = arB.take([128, 2, 128], BF16)
            tmpB = arB.take([128, 2, 128], F32)
            tmpB2 = arB.take([128, 2, 128], F32)
            small = arB.take([128, 24], F32)
            st = arB.take([128, 48], F32)
            stg = otmps[0].rearrange("p (a b) -> p a b", b=128)
            stg2 = otmps[1].rearrange("p (a b) -> p a b", b=128)
            dma("sp", stg[:, 0:8, :], poolM_d[0:8].rearrange("e p n -> p e n"), w=["stg"], key="stg")
            dma("sp", stg2[:, 0:4, :], poolM_d[8:12].rearrange("e p n -> p e n"), w=["stg2a"], key="stg2a")
            dma("sp", stg2[:, 4:8, :], sguwT_d[l].rearrange("g s t -> s g t"), w=["stg2b"], key="stg2b")
            dma("sp", rmf, tri_d, w=["rmf"], key="rmf")
            cp("dve", poolMb[:, 0:8, :], stg[:, 0:8, :], r=["stg"], w=["poolMb"])
            cp("dve", poolMb[:, 8:12, :], stg2[:, 0:4, :], r=["stg2a"], w=["poolMb"])
            tt("dve", sguwb, stg2[:, 4:8, :], rmf.unsqueeze(1).to_broadcast([128, 4, 128]), ALU.mult,
               r=["stg2b", "rmf"], w=["sguwb"])
            dma("sp", rmf, rm_d, r=[], w=["rmf"], key="rmf")
            dma("sp", g128, g128_d, w=["g128"], key="g128")
            for a in range(2):
                for c2 in range(2):
                    dma("sp", biasB[64 * a:64 * a + 64, c2, :], sgub_d[l, 2 * c2 + a].partition_broadcast(64),
                        w=["biasB"], key="biasB%d%d" % (a, c2))
            pws = tmpB2[:, :, 0:64]
            for a in range(2):
                dma("sp", pws[64 * a:64 * a + 64, :, :],
                    poolw_d[l].rearrange("(c2 a) ci d -> a ci c2 d", a=2)[a],
                    w=["pws"], key="pws%d" % a)
            cp("dve", poolwb, pws, r=["pws"], w=["poolwb"])
            memset("pool", Sst, 0.0, w=["Sst"])
            memset("pool", Sbf, 0.0, w=["Sbf"])
            memset("pool", small, 0.0, w=["smallz"])
            memset("pool", st, 0.0, w=["stz"])

            sgug = vecs[:, 16 + l * 2:18 + l * 2]
            pscale = vecs[:, 20 + l * 2:22 + l * 2]
            retg = vecs[:, 24 + l * 2:26 + l * 2]
            vg4 = vecs[:, 29:33]
            xi4 = vecs[:, 33:37]

            nfm = 0

            def load_x(s_):
                for j_ in range(4):
                    t_ = 4 * s_ + j_
                    dma("sp", xts[t_ % NXS], xin_tile(t_), r=xin_keys(t_), w=["xt%d" % (t_ % NXS)],
                        key="xt%d" % (t_ % NXS))

            def p3_norm_a(s_, j_):
                t_ = 4 * s_ + j_
                norm_a(xts[t_ % NXS], "xt%d" % (t_ % NXS), hs[j_ % 2], "h%d" % (j_ % 2), junk, small, j_)

            def p3_norm_b(s_, j_):
                norm_b(hs[j_ % 2], "h%d" % (j_ % 2), hT3[s_ % 2], "hT%d" % (s_ % 2), j_, 0)

            load_x(0)
            y_carry = []
            if NORM_AHEAD:
                for j in range(4):
                    p3_norm_a(0, j)
                    p3_norm_b(0, j)
            for s in range(NST):
                hT = hT3[s % 2]
                hTkey = "hT%d" % (s % 2)
                if not NORM_AHEAD:
                    for j in range(4):
                        p3_norm_a(s, j)
                        p3_norm_b(s, j)
                P.rec = rec_f = []
                for si, kind in enumerate(("bu", "bg", "cg", "dq", "dk", "dg")):
                    for c in range(2):
                        bi = 1 + nfm % 2
                        nfm += 1
                        pk = "ps%d" % bi
                        col0 = (4 + si) * 256 + c * 128
                        for k in range(8):
                            mm(bk(bi), win_bf[:, k, col0:col0 + 128], hT[:, k, :], start=(k == 0), stop=(k == 7),
                               r=["win_bf", hTkey], w=[pk])
                        if kind == "bu":
                            cp("act", buT[:, c, :], bk(bi), r=[pk], w=["buT"])
                        elif kind == "dq":
                            cp("dve", dqT[:, c, :], bk(bi), r=[pk], w=["dqT"])
                        elif kind == "dk":
                            cp("act", dkT[:, c, :], bk(bi), r=[pk], w=["dkT"])
                        elif kind == "bg":
                            gate_evac(bk(bi), pk, ttmp, gbT[:, c, :], "gbT")
                        elif kind == "cg":
                            gate_evac(bk(bi), pk, ttmp, gcT[:, c, :], "gcT")
                        else:
                            gate_evac(bk(bi), pk, ttmp, gdT[:, c, :], "gdT")
                P.rec = None
                P.replay(interleave([rec_f, y_carry]))
                y_carry = []
                if s + 1 < NST:
                    load_x(s + 1)
                y_prev = []
                for j in range(4):
                    if p3level < 1:
                        break
                    t = 4 * s + j
                    J = slice(j * 128, (j + 1) * 128)
                    xs = xts[t % NXS]
                    xkey = "xt%d" % (t % NXS)
                    if NORM_AHEAD and s + 1 < NST:
                        p3_norm_a(s + 1, j)
                    for wbi, c0 in ((3, 10 * 256), (4, 12 * 256)):
                        for k in range(8):
                            mm(bk(wbi), hT[:, k, J], win_bf[:, k, c0:c0 + 512], start=(k == 0), stop=(k == 7),
                               r=[hTkey, "win_bf"], w=["ps%d" % wbi])
                    P.rec = rec_b = []
                    bv = bk(3)[:, 0:256]
                    red(st[:, 0:1], bv, r=["ps3", "stz"], w=["st0"])
                    act(junkB, bv, AF.Square, r=["ps3", "stz"], w=["junkB", "st1"], accum=st[:, 1:2])
                    ts(st[:, 2:3], st[:, 0:1], 1.0 / 256, None, ALU.mult, r=["st0"], w=["st2"])
                    tt("dve", st[:, 3:4], st[:, 2:3], st[:, 2:3], ALU.mult, r=["st2"], w=["st3"])
                    stt(st[:, 4:5], st[:, 1:2], 1.0 / 256, st[:, 3:4], ALU.mult, ALU.subtract,
                        r=["st1", "st3"], w=["st4"])
                    ts(st[:, 5:6], st[:, 4:5], EPS, None, ALU.add, r=["st4"], w=["st5"])
                    tt("pool", st[:, 6:7], st[:, 5:6], m05, ALU.pow, r=["st5", "vecs"], w=["st6"])
                    ts(vn, bv, st[:, 2:3], st[:, 6:7], ALU.subtract, ALU.mult, r=["ps3", "st2", "st6"], w=["vn"])
                    psBm = bk(5)[:, 0:256].rearrange("p (c t) -> p c t", t=128)
                    for g in range(4):
                        a0 = 64 * (g % 2)
                        mm(psBm[a0:a0 + 64, g // 2, :], vn[:, 64 * g:64 * g + 64], sguwb[:, g, :],
                           r=["vn", "sguwb"], w=["ps5"])
                    for c in range(2):
                        stt(tmpB[:, c, :], psBm[:, c, :], sgug[:, c:c + 1], biasB[:, c, :], ALU.mult, ALU.add,
                            r=["ps5", "vecs", "biasB"], w=["tmpB"])
                    tt("dve", tmpB2, tmpB, buT[:, :, J], ALU.mult, r=["tmpB", "buT"], w=["tmpB2"])
                    tt("dve", yT[:, 0:2, J], tmpB2, gbT[:, :, J], ALU.mult, r=["tmpB2", "gbT"], w=["yT%d" % j])
                    P.rec = rec_c = []
                    if p3level >= 2:
                        xc = xcs[t % 2]
                        xck = "xc%d" % (t % 2)
                        xcp = xcs[(t - 1) % 2]
                        xcpk = "xc%d" % ((t - 1) % 2)
                        cp("act", xc, bk(3)[:, 256:512], r=["ps3"], w=[xck])
                        psC = bk(5)[:, 256:512].rearrange("p (c t) -> p c t", t=128)
                        for g in range(4):
                            a0 = 64 * (g % 2)
                            o_ = psC[a0:a0 + 64, g // 2, :]
                            if t == 0:
                                mm(o_, xc[:, 64 * g:64 * g + 64], poolMb[:, 8 + g, :], r=[xck, "poolMb"], w=["ps5"])
                            else:
                                mm(o_, xc[:, 64 * g:64 * g + 64], poolMb[:, g, :], start=True, stop=False,
                                   r=[xck, "poolMb"], w=["ps5"])
                                mm(o_, xcp[:, 64 * g:64 * g + 64], poolMb[:, 4 + g, :], start=False, stop=True,
                                   r=[xcpk, "poolMb"], w=["ps5"])
                        cp("act", pooledT, psC, r=["ps5"], w=["pooledT"])
                        for g in range(4):
                            a0 = 64 * (g % 2)
                            mm(psC[a0:a0 + 64, g // 2, :], poolwb[a0:a0 + 64, g // 2, :], pooledT[a0:a0 + 64, g // 2, :],
                               r=["poolwb", "pooledT"], w=["ps5"])
                        for c in range(2):
                            stt(yT[:, 2 + c, J], psC[:, c, :], pscale[:, c:c + 1], gcT[:, c, J], ALU.mult, ALU.mult,
                                r=["ps5", "vecs", "gcT"], w=["yT%d" % j])
                    P.rec = rec_d = []
                    if p3level >= 3:
                        cp("act", dkt, bk(4)[:, 0:256], r=["ps4"], w=["dkt"])
                        tt("dve", dvp.rearrange("p (h e) -> p h e", e=64),
                           bk(4)[:, 256:512].rearrange("p (h e) -> p h e", e=64),
                           vg4.unsqueeze(2).to_broadcast([128, 4, 64]), ALU.mult, r=["ps4", "vecs"], w=["dvp"])
                        psDsE = bk(6)[:, 0:256].rearrange("p (c i) -> p c i", i=128)
                        psDsO = bk(7)[:, 256:512].rearrange("p (c i) -> p c i", i=128)
                        ATv = AT.rearrange("p (c a) i -> p a c i", a=2)
                        for h in range(4):
                            a0 = 64 * (h % 2)
                            dst = psDsE if h % 2 == 0 else psDsO
                            mm(dst[:, h // 2, :], dkT[a0:a0 + 64, h // 2, J], dqT[a0:a0 + 64, h // 2, J],
                               r=["dkT", "dqT"], w=["ps6" if h % 2 == 0 else "ps7"])
                        tt("dve", ATv[:, 0], psDsE, rmf.unsqueeze(1).to_broadcast([128, 2, 128]), ALU.mult,
                           r=["ps6", "rmf"], w=["AT"])
                        tt("dve", ATv[:, 1], psDsO, rmf.unsqueeze(1).to_broadcast([128, 2, 128]), ALU.mult,
                           r=["ps7", "rmf"], w=["AT"])
                        psDo = bk(7)[:, 0:256]
                        for h in range(4):
                            a0 = 64 * (h % 2)
                            mm(psDo[:, 64 * h:64 * h + 64], AT[:, h, :], dvp[:, 64 * h:64 * h + 64],
                               start=True, stop=False, r=["AT", "dvp"], w=["ps7"])
                            mm(psDo[:, 64 * h:64 * h + 64], dqT[a0:a0 + 64, h // 2, J],
                               Sbf[a0:a0 + 64, (h // 2) * 128 + a0:(h // 2) * 128 + a0 + 64],
                               start=False, stop=True, r=["dqT", "Sbf"], w=["ps7"])
                        psKv = bk(6)[:, 256:512]
                        for c in range(2):
                            mm(psKv[:, c * 128:(c + 1) * 128], dkt[:, c * 128:(c + 1) * 128], dvp[:, c * 128:(c + 1) * 128],
                               r=["dkt", "dvp"], w=["ps6"])
                        ofv = of_.rearrange("p (h e) -> p h e", e=64)
                        tt("dve", ofv, psDo.rearrange("p (h e) -> p h e", e=64),
                           xi4.unsqueeze(2).to_broadcast([128, 4, 64]), ALU.mult, r=["ps7", "vecs"], w=["of"])
                        stt(Sst, psKv, 0.125, Sst, ALU.mult, ALU.add, r=["ps6", "Sst"], w=["Sst"])
                        tt("dve", Sst, Sst, g128, ALU.mult, r=["Sst", "g128"], w=["Sst"])
                        cp("act", Sbf, Sst, r=["Sst"], w=["Sbf"])
                        red(st[:, 8:12], ofv, r=["of"], w=["st8"])
                        act(sq, of_, AF.Square, r=["of"], w=["sq"])
                        red(st[:, 12:16], sq.rearrange("p (h e) -> p h e", e=64), r=["sq"], w=["st12"])
                        ts(st[:, 16:20], st[:, 8:12], 1.0 / 64, None, ALU.mult, r=["st8"], w=["st16"])
                        tt("dve", st[:, 20:24], st[:, 16:20], st[:, 16:20], ALU.mult, r=["st16"], w=["st20"])
                        stt(st[:, 24:28], st[:, 12:16], 1.0 / 64, st[:, 20:24], ALU.mult, ALU.subtract,
                            r=["st12", "st20"], w=["st24"])
                        ts(st[:, 28:32], st[:, 24:28], EPS, None, ALU.add, r=["st24"], w=["st28"])
                        tt("pool", st[:, 32:36], st[:, 28:32], m05.to_broadcast([128, 4]), ALU.pow,
                           r=["st28", "vecs"], w=["st32"])
                        tt("dve", ofv, ofv, st[:, 16:20].unsqueeze(2).to_broadcast([128, 4, 64]), ALU.subtract,
                           r=["of", "st16"], w=["of"])
                        tt("dve", onb.rearrange("p (h e) -> p h e", e=64), ofv,
                           st[:, 32:36].unsqueeze(2).to_broadcast([128, 4, 64]), ALU.mult, r=["of", "st32"], w=["onb"])
                        psDt = bkbf(7)[:, 0:256].rearrange("p (c t) -> p c t", t=128)
                        for c in range(2):
                            tr(psDt[:, c, :], onb[:, c * 128:(c + 1) * 128], identb[:], r=["onb", "identb"], w=["ps7"])
                        for c in range(2):
                            stt(yT[:, 4 + c, J], psDt[:, c, :], retg[:, c:c + 1], gdT[:, c, J], ALU.mult, ALU.mult,
                                r=["ps7", "vecs", "gdT"], w=["yT%d" % j])
                    P.rec = rec_n = []
                    if NORM_AHEAD and s + 1 < NST:
                        p3_norm_b(s + 1, j)
                    P.rec = rec_y = []
                    if p3level >= 4:
                        yb = 3 if j == 3 else 1
                        for n_ in range(2):
                            wbi = yb + n_
                            for kc in range(8):
                                if kc < 2:
                                    lhs = yaT[:, kc, s * 512 + j * 128:s * 512 + (j + 1) * 128]
                                    rk = "yaT"
                                else:
                                    lhs = yT[:, kc - 2, J]
                                    rk = "yT%d" % j
                                mm(bk(wbi), lhs, wout_bf[:, kc, n_ * 512:(n_ + 1) * 512], start=(kc == 0), stop=(kc == 7),
                                   r=[rk, "wout_bf"], w=["ps%d" % wbi])
                        act(junkY[:, 0:512], bk(yb), AF.Square, r=["ps%d" % yb, "stz"], w=["junkY0", "st40"], accum=st[:, 40:41])
                        act(junkY[:, 512:1024], bk(yb + 1), AF.Square, r=["ps%d" % (yb + 1), "stz"], w=["junkY1", "st41"], accum=st[:, 41:42])
                        tt("dve", st[:, 42:43], st[:, 40:41], st[:, 41:42], ALU.add, r=["st40", "st41"], w=["st42"])
                        ts(st[:, 43:44], st[:, 42:43], 1.0 / D, EPS, ALU.mult, ALU.add, r=["st42"], w=["st43"])
                        tt("pool", st[:, 44:45], st[:, 43:44], m05, ALU.pow, r=["st43", "vecs"], w=["st44"])
                        ot = otmps[t % 2]
                        otk = "otmp%d" % (t % 2)
                        for n_ in range(2):
                            stt(ot[:, n_ * 512:(n_ + 1) * 512], bk(yb + n_), st[:, 44:45], postg[:, n_ * 512:(n_ + 1) * 512],
                                ALU.mult, ALU.mult, r=["ps%d" % (yb + n_), "st44", "postg"], w=[otk])
                        tt("dve", ot, ot, xs, ALU.add, r=[otk, xkey], w=[otk])
                        dma("sp", out_t[t], ot, r=[otk], w=["od%d" % t], key=otk)
                    P.rec = None
                    P.replay(interleave([rec_b, rec_c, rec_d, rec_n, y_prev]))
                    y_prev = rec_y
                y_carry = y_prev
            P.replay(y_carry)

        final_keys = ["od%d" % t for t in range(NT)] + ["dbg"]
        names = P.finalize(final_wait_keys=final_keys)
        sems = {n: es.enter_context(nc.semaphore(n)) for n in names}
        P.emit(nc, sems)
    return nc


def _constants():
    c = {}
    c["ident"] = np.eye(128, dtype=np.float32)
    s_ = np.arange(128)[:, None]
    t_ = np.arange(128)[None, :]
    c["triT"] = (s_ <= t_).astype(np.float32)
    pm = np.zeros((12, 128, 128), np.float64)
    for g, p in enumerate(POOLS):
        cur = ((s_ >= t_ - p + 1) & (s_ <= t_)).astype(np.float64) / p - (s_ == t_)
        prev = ((s_ >= t_ - p + 129) & (s_ <= 127)).astype(np.float64) / p
        cnt = np.minimum(t_ + 1, p).astype(np.float64)
        first = ((s_ >= np.maximum(t_ - p + 1, 0)) & (s_ <= t_)).astype(np.float64) / cnt - (s_ == t_)
        pm[g], pm[4 + g], pm[8 + g] = cur, prev, first
    c["poolM"] = pm.astype(np.float32)
    et = np.zeros((12, 128, 2, 128), np.float64)
    k_ = np.arange(128)[:, None].astype(np.float64)
    q_ = np.arange(128)[None, :].astype(np.float64)
    for h in range(4):
        slope = 2.0 ** (-8.0 * (h + 1) / 4)
        for g, dil in enumerate(DILS):
            dprev = q_ + 128 - k_
            dcur = q_ - k_
            et[h * 3 + g, :, 0, :] = np.where(dprev <= 128, np.exp(-slope * dprev * dil), 0.0)
            et[h * 3 + g, :, 1, :] = np.where(dcur >= 0, np.exp(-slope * np.maximum(dcur, 0) * dil), 0.0)
    c["etab"] = et.reshape(12, 128, 256).astype(np.float32)
    c["rm"] = ((t_ >= s_).astype(np.float32) * 0.125).astype(np.float32)
    log_g = np.log(1.0 - np.exp2(-5.0 - np.arange(4, dtype=np.float64)))
    g128 = np.zeros((128, 2, 128), np.float64)
    for c2 in range(2):
        g128[0:64, c2, :] = np.exp(log_g[2 * c2] * 128)
        g128[64:128, c2, :] = np.exp(log_g[2 * c2 + 1] * 128)
    c["g128"] = g128.reshape(128, 256).astype(np.float32)
    p_ = np.arange(128, dtype=np.float64)[:, None]
    c["vg4"] = np.exp(-log_g[None, :] * (p_ + 1)).astype(np.float32)
    c["xi4"] = np.exp(log_g[None, :] * (p_ + 1)).astype(np.float32)
    return c


_CACHE = {}


def kernel(x, pre_g, w_in, sgu_g, sgu_w, sgu_b, pool_w, pool_scale, ret_g, w_out, post_g):
    f = lambda a: np.ascontiguousarray(np.asarray(a, dtype=np.float32))
    x, pre_g, w_in, sgu_g, sgu_w, sgu_b = f(x), f(pre_g), f(w_in), f(sgu_g), f(sgu_w), f(sgu_b)
    pool_w, pool_scale, ret_g, w_out, post_g = f(pool_w), f(pool_scale), f(ret_g), f(w_out), f(post_g)
    c = _constants()
    vecs = np.zeros((128, NV), np.float32)
    vecs[:, 0:16] = pre_g.reshape(DEPTH, 8, 128).transpose(2, 0, 1).reshape(128, 16)
    vecs[:, 16:20] = sgu_g.reshape(DEPTH, 2, 128).transpose(2, 0, 1).reshape(128, 4)
    vecs[:, 20:24] = pool_scale.reshape(DEPTH, 2, 128).transpose(2, 0, 1).reshape(128, 4)
    vecs[:, 24:28] = ret_g.reshape(DEPTH, 2, 128).transpose(2, 0, 1).reshape(128, 4)
    vecs[:, 28] = -0.5
    vecs[:, 29:33] = c["vg4"]
    vecs[:, 33:37] = c["xi4"]
    shared = {
        "w_in": w_in, "w_out": w_out,
        "sgu_wT": np.ascontiguousarray(sgu_w.transpose(0, 1, 3, 2)),
        "sgu_b": sgu_b, "pool_w": pool_w, "post_g": post_g, "vecs": vecs,
        "ident": c["ident"], "triT": c["triT"], "poolM": c["poolM"], "etab": c["etab"],
        "rm": c["rm"], "g128": c["g128"],
    }
    if "nc" not in _CACHE:
        _CACHE["nc"] = build_program()
    nc = _CACHE["nc"]
    n = x.shape[0]
    in_maps = [dict(shared, x=np.ascontiguousarray(x[i])) for i in range(n)]
    res = run_bass_kernel_spmd(nc, in_maps, core_ids=list(range(n)))
    return np.stack([np.asarray(r["out"], dtype=np.float32) for r in res.results], axis=0)
```
